# Optimizing a Trainium2 kernel written in Bass

```python
import math
import jax, jax.numpy as jnp
from jax import lax
import numpy as np

D_MODEL = 1024
BATCH = 16
SEQ = 2048
DEPTH = 2

D_MIX = D_MODEL
W_POOL = D_MIX // 4
W_SCONV = D_MIX // 4
W_CONF = D_MIX // 4
W_ATTN = D_MIX // 4
POOL_WINDOWS = (2, 4, 8, 16)
N_POOL_GROUPS = len(POOL_WINDOWS)
POOL_GROUP = W_POOL // N_POOL_GROUPS
SCONV_K = 3
CONF_K = 31
N_HEADS = 4
HEAD_V = W_ATTN // N_HEADS
HEAD_QK = HEAD_V // 2
SPLIT_SIZES = (W_POOL, W_SCONV, W_SCONV, W_SCONV, W_CONF, W_CONF, W_ATTN, W_ATTN, W_ATTN)
D_IN = sum(SPLIT_SIZES)
D_FF = ((8 * D_MODEL + 3 * 256 - 1) // (3 * 256)) * 256
REL_BUCKETS = 32
REL_MAX_EXACT = REL_BUCKETS // 2
REL_MAX_DIST = 128
Q_BLOCK = 128
EPS = 1e-6
NEG_INF = -1e30

kernel_name = 'hybrid_parallel_pool_conv_conformer_diffattn'


def rms_norm(x, g):
    xf = x.astype(jnp.float32)
    y = xf * lax.rsqrt(jnp.mean(xf * xf, axis=-1, keepdims=True) + EPS)
    return (y * g.astype(jnp.float32)).astype(x.dtype)


def layer_norm(x, g, b):
    xf = x.astype(jnp.float32)
    mu = jnp.mean(xf, axis=-1, keepdims=True)
    xc = xf - mu
    y = xc * lax.rsqrt(jnp.mean(xc * xc, axis=-1, keepdims=True) + EPS)
    return (y * g.astype(jnp.float32) + b.astype(jnp.float32)).astype(x.dtype)


def causal_dwconv(u, w):
    k = w.shape[0]
    return lax.conv_general_dilated(
        u, w[:, None, :].astype(u.dtype), window_strides=(1,), padding=[(k - 1, 0)],
        dimension_numbers=('NWC', 'WIO', 'NWC'), feature_group_count=u.shape[-1])


def pool_mixer(u, pool_w, pool_scale):
    b, s, _ = u.shape
    c = jnp.cumsum(u.astype(jnp.float32), axis=1)
    pos1 = jnp.arange(1, s + 1)
    parts = []
    for g, win in enumerate(POOL_WINDOWS):
        cg = c[..., g * POOL_GROUP:(g + 1) * POOL_GROUP]
        shifted = jnp.pad(cg, ((0, 0), (win, 0), (0, 0)))[:, :s]
        cnt = jnp.minimum(pos1, win).astype(jnp.float32)[:, None]
        parts.append((cg - shifted) / cnt)
    pooled = jnp.concatenate(parts, axis=-1).astype(u.dtype) - u
    y = jnp.einsum('bsgc,gcd->bsgd', pooled.reshape(b, s, N_POOL_GROUPS, POOL_GROUP), pool_w)
    return y.reshape(b, s, W_POOL) * pool_scale


def rel_bucket(n):
    nf = jnp.maximum(n, 1).astype(jnp.float32)
    large = REL_MAX_EXACT + (jnp.log(nf / REL_MAX_EXACT) / math.log(REL_MAX_DIST / REL_MAX_EXACT)
                             * (REL_BUCKETS - REL_MAX_EXACT)).astype(jnp.int32)
    large = jnp.minimum(large, REL_BUCKETS - 1)
    return jnp.where(n < REL_MAX_EXACT, n, large)


def diff_attention(q, k, v, q_norm_g, k_norm_g, lam, lam_init, subln_g, rel_bias):
    b, s = q.shape[:2]
    q = rms_norm(q, q_norm_g)
    k = rms_norm(k, k_norm_g)
    scale = HEAD_QK ** -0.5
    nb = s // Q_BLOCK
    qb = q.reshape(b, nb, Q_BLOCK, N_HEADS, 2, HEAD_QK).transpose(1, 0, 2, 3, 4, 5)
    starts = jnp.arange(nb, dtype=jnp.int32) * Q_BLOCK
    kpos = jnp.arange(s, dtype=jnp.int32)

    def block(args):
        qblk, start = args
        qpos = start + jnp.arange(Q_BLOCK, dtype=jnp.int32)
        dist = qpos[:, None] - kpos[None, :]
        bias = rel_bias[rel_bucket(jnp.maximum(dist, 0))].astype(jnp.float32)
        bias = bias.transpose(2, 0, 1)[:, None]
        sc = jnp.einsum('bqhmd,bkhmd->bhmqk', qblk, k).astype(jnp.float32) * scale + bias
        sc = jnp.where(dist >= 0, sc, NEG_INF)
        p = jax.nn.softmax(sc, axis=-1)
        a = p[:, :, 0] - lam * p[:, :, 1]
        return jnp.einsum('bhqk,bkhd->bqhd', a.astype(v.dtype), v)

    o = lax.map(block, (qb, starts))
    o = o.transpose(1, 0, 2, 3, 4).reshape(b, s, N_HEADS, HEAD_V)
    o = rms_norm(o, subln_g) * (1.0 - lam_init)
    return o.reshape(b, s, W_ATTN)


def hybrid_layer(x, norm1_g, w_in, pool_w, pool_scale, sconv_w, conf_dw_w, conf_dw_b,
                 conf_ln_g, conf_ln_b, q_norm_g, k_norm_g, lam_q1, lam_k1, lam_q2, lam_k2,
                 subln_g, w_out, norm2_g, w_gate, w_up, w_down, rel_bias, lam_init):
    b, s, _ = x.shape
    h = rms_norm(x, norm1_g)
    z = h @ w_in
    idx = [int(i) for i in np.cumsum(SPLIT_SIZES)[:-1]]
    z_pool, s_b, s_c, s_h, c_a, c_g, zq, zk, zv = jnp.split(z, idx, axis=-1)
    y_pool = pool_mixer(z_pool, pool_w, pool_scale)
    y_sconv = s_b * causal_dwconv(s_c * s_h, sconv_w)
    u = c_a * jax.nn.sigmoid(c_g)
    u = causal_dwconv(u, conf_dw_w) + conf_dw_b
    y_conf = jax.nn.silu(layer_norm(u, conf_ln_g, conf_ln_b))
    lam = (jnp.exp(jnp.sum(lam_q1.astype(jnp.float32) * lam_k1.astype(jnp.float32)))
           - jnp.exp(jnp.sum(lam_q2.astype(jnp.float32) * lam_k2.astype(jnp.float32))) + lam_init)
    q = zq.reshape(b, s, N_HEADS, 2, HEAD_QK)
    k = zk.reshape(b, s, N_HEADS, 2, HEAD_QK)
    v = zv.reshape(b, s, N_HEADS, HEAD_V)
    y_attn = diff_attention(q, k, v, q_norm_g, k_norm_g, lam, lam_init, subln_g, rel_bias)
    mix = jnp.concatenate([y_pool, y_sconv, y_conf, y_attn], axis=-1)
    x = x + mix @ w_out
    h2 = rms_norm(x, norm2_g)
    x = x + (jax.nn.silu(h2 @ w_gate) * (h2 @ w_up)) @ w_down
    return x


def setup_inputs(seed: int = 0) -> dict:
    key = jax.random.key(seed)
    ks = jax.random.split(key, 24)
    f32 = jnp.float32
    nrm = lambda k, shape, sc: jax.random.normal(k, shape, f32) * sc
    L = DEPTH
    return {
        'x': nrm(ks[0], (BATCH, SEQ, D_MODEL), 1.0),
        'norm1_g': 1.0 + nrm(ks[1], (L, D_MODEL), 0.05),
        'w_in': nrm(ks[2], (L, D_MODEL, D_IN), D_MODEL ** -0.5),
        'pool_w': nrm(ks[3], (L, N_POOL_GROUPS, POOL_GROUP, POOL_GROUP), POOL_GROUP ** -0.5),
        'pool_scale': 1.0 + nrm(ks[4], (L, W_POOL), 0.1),
        'sconv_w': nrm(ks[5], (L, SCONV_K, W_SCONV), SCONV_K ** -0.5),
        'conf_dw_w': nrm(ks[6], (L, CONF_K, W_CONF), CONF_K ** -0.5),
        'conf_dw_b': nrm(ks[7], (L, W_CONF), 0.02),
        'conf_ln_g': 1.0 + nrm(ks[8], (L, W_CONF), 0.05),
        'conf_ln_b': nrm(ks[9], (L, W_CONF), 0.02),
        'q_norm_g': 1.0 + nrm(ks[10], (L, HEAD_QK), 0.05),
        'k_norm_g': 1.0 + nrm(ks[11], (L, HEAD_QK), 0.05),
        'lam_q1': nrm(ks[12], (L, HEAD_QK), 0.1),
        'lam_k1': nrm(ks[13], (L, HEAD_QK), 0.1),
        'lam_q2': nrm(ks[14], (L, HEAD_QK), 0.1),
        'lam_k2': nrm(ks[15], (L, HEAD_QK), 0.1),
        'subln_g': 1.0 + nrm(ks[16], (L, HEAD_V), 0.05),
        'w_out': nrm(ks[17], (L, D_MIX, D_MODEL), D_MIX ** -0.5),
        'norm2_g': 1.0 + nrm(ks[18], (L, D_MODEL), 0.05),
        'w_gate': nrm(ks[19], (L, D_MODEL, D_FF), D_MODEL ** -0.5),
        'w_up': nrm(ks[20], (L, D_MODEL, D_FF), D_MODEL ** -0.5),
        'w_down': nrm(ks[21], (L, D_FF, D_MODEL), D_FF ** -0.5),
        'rel_bias': nrm(ks[22], (REL_BUCKETS, N_HEADS), 0.5),
    }


def reference(x, norm1_g, w_in, pool_w, pool_scale, sconv_w, conf_dw_w, conf_dw_b, conf_ln_g,
              conf_ln_b, q_norm_g, k_norm_g, lam_q1, lam_k1, lam_q2, lam_k2, subln_g, w_out,
              norm2_g, w_gate, w_up, w_down, rel_bias):
    for l in range(DEPTH):
        lam_init = 0.8 - 0.6 * math.exp(-0.3 * l)
        x = hybrid_layer(x, norm1_g[l], w_in[l], pool_w[l], pool_scale[l], sconv_w[l],
                         conf_dw_w[l], conf_dw_b[l], conf_ln_g[l], conf_ln_b[l], q_norm_g[l],
                         k_norm_g[l], lam_q1[l], lam_k1[l], lam_q2[l], lam_k2[l], subln_g[l],
                         w_out[l], norm2_g[l], w_gate[l], w_up[l], w_down[l], rel_bias, lam_init)
    return x
```

```python
import math
from contextlib import ExitStack

import numpy as np
import concourse.bass as bass
import concourse.mybir as mybir
from concourse.bass_utils import run_bass_kernel_spmd

F32 = mybir.dt.float32
BF16 = mybir.dt.bfloat16
AF = mybir.ActivationFunctionType
ALU = mybir.AluOpType

D = 1024
KT = D // 128
D_IN = 2304
D_FF = 2816
NG_FF = D_FF // 256
N_HEADS = 4
HEAD_V = 64
HEAD_QK = 32
CONF_K = 31
EPS = 1e-6
C = 512
HALO = 32
TW = C + HALO
GW = 640
EW = 768
FAR_MIN = 113
N_CORES = 8
RING = 4
NT = 8


def lam_init_of(l):
    return 0.8 - 0.6 * math.exp(-0.3 * l)


def cpack_layout(depth):
    off = {}
    n = 0

    def add(name, w):
        nonlocal n
        off[name] = n
        n += w

    add("ones128", 128)
    add("bd32", 128)
    add("sel2", 128)
    add("bd64", 128)
    add("rb31", 4)
    add("invc", 32)
    for l in range(depth):
        for name, w in (("n1g", 8), ("n2g", 8), ("pscale", 2), ("sconv", 6), ("cdw", 62), ("cdb", 2),
                        ("clg", 2), ("clb", 2), ("qg0", 1), ("qg1", 1), ("kg", 1), ("sg", 1), ("lq1", 32), ("lk1", 32),
                        ("lq2", 32), ("lk2", 32), ("bd", 256), ("neglam", 1), ("ltmp", 36)):
            add(f"{name}{l}", w)
    return off, n


def rel_bucket_np(n):
    n = np.asarray(n, dtype=np.int64)
    nf = np.maximum(n, 1).astype(np.float32)
    large = 16 + (np.log(nf / np.float32(16)) / np.float32(math.log(128 / 16)) * np.float32(16)).astype(np.int32)
    large = np.minimum(large, 31)
    return np.where(n < 16, n, large)


def build_cpack(inp, depth):
    off, n = cpack_layout(depth)
    cp = np.zeros((128, n), np.float32)
    p = np.arange(128)
    cp[:, off["ones128"]:off["ones128"] + 128] = 1.0
    bd32 = (p[:, None] // 32 == p[None, :] // 32).astype(np.float32)
    cp[:, off["bd32"]:off["bd32"] + 128] = bd32
    cp[64, off["sel2"]:off["sel2"] + 64] = 1.0
    cp[63, off["sel2"] + 64:off["sel2"] + 128] = 1.0
    cp[:, off["bd64"]:off["bd64"] + 128] = (p[:, None] // 64 == p[None, :] // 64).astype(np.float32)
    cp[:, off["rb31"]:off["rb31"] + 4] = inp["rel_bias"][31][None, :]
    t = np.arange(16)
    for hf in range(2):
        win = np.where(p < 64, 2 ** (2 * hf + 1), 2 ** (2 * hf + 2))
        cp[:, off["invc"] + hf * 16: off["invc"] + hf * 16 + 16] = \
            1.0 / np.minimum(t[None, :] + 1, win[:, None]).astype(np.float32)
    for l in range(depth):
        o = lambda nm: off[f"{nm}{l}"]
        cp[:, o("n1g"):o("n1g") + 8] = inp["norm1_g"][l].reshape(8, 128).T
        cp[:, o("n2g"):o("n2g") + 8] = inp["norm2_g"][l].reshape(8, 128).T
        cp[:, o("pscale"):o("pscale") + 2] = inp["pool_scale"][l].reshape(2, 128).T
        cp[:, o("sconv"):o("sconv") + 6] = inp["sconv_w"][l].reshape(3, 2, 128).transpose(2, 1, 0).reshape(128, 6)
        cp[:, o("cdw"):o("cdw") + 62] = inp["conf_dw_w"][l].reshape(31, 2, 128).transpose(2, 1, 0).reshape(128, 62)
        cp[:, o("cdb"):o("cdb") + 2] = inp["conf_dw_b"][l].reshape(2, 128).T
        cp[:, o("clg"):o("clg") + 2] = inp["conf_ln_g"][l].reshape(2, 128).T
        cp[:, o("clb"):o("clb") + 2] = inp["conf_ln_b"][l].reshape(2, 128).T
        cp[:, o("qg0")] = np.where((p // 32) % 2 == 0, inp["q_norm_g"][l][p % 32], 0.0)
        cp[:, o("qg1")] = np.where((p // 32) % 2 == 1, inp["q_norm_g"][l][p % 32], 0.0)
        cp[:, o("kg")] = inp["k_norm_g"][l][p % 32]
        cp[:, o("sg")] = inp["subln_g"][l][p % 64]
        lamv = {"lq1": inp["lam_q1"], "lk1": inp["lam_k1"], "lq2": inp["lam_q2"], "lk2": inp["lam_k2"]}
        for nm in ("lq1", "lk1", "lq2", "lk2"):
            cp[:, o(nm):o(nm) + 32] = lamv[nm][l][None, :]
        for hf in range(2):
            for g in range(2):
                cp[g * 64:(g + 1) * 64, o("bd") + hf * 128 + g * 64: o("bd") + hf * 128 + (g + 1) * 64] = \
                    inp["pool_w"][l][2 * hf + g]
    return cp


def build_bias_tables(inp):
    oh = np.zeros((32, EW), np.float32)
    d = np.arange(EW - 127)
    b = rel_bucket_np(d)
    oh[b, d + 127] = 1.0
    rbrep = np.repeat(inp["rel_bias"].astype(np.float32)[:, :, None], 128, axis=2).reshape(32, 4 * 128)
    return oh, rbrep


class Op:
    __slots__ = ("eng", "fn", "deps", "needs", "sem", "val", "is_dma", "label")


class Prog:
    ENGS = ("pe", "act", "dve", "pool", "sp")

    def __init__(self):
        self.ops = {e: [] for e in self.ENGS}
        self.lastw = {}
        self.readers = {}
        self.dma_count = {}
        self.label = "setup"

    def op(self, eng, fn, reads=(), writes=(), dma=None):
        o = Op()
        o.label = self.label
        o.eng, o.fn, o.deps, o.needs, o.sem, o.val = eng, fn, set(), False, None, None
        o.is_dma = dma is not None
        for k in reads:
            w = self.lastw.get(k)
            if w is not None:
                o.deps.add(w)
        for k in writes:
            w = self.lastw.get(k)
            if w is not None:
                o.deps.add(w)
            for r in self.readers.get(k, ()):
                o.deps.add(r)
        if eng == "pe":
            o.deps = {d for d in o.deps if d.eng != "pe" or d.is_dma}
        for k in reads:
            self.readers.setdefault(k, []).append(o)
        for k in writes:
            self.lastw[k] = o
            self.readers[k] = []
        for d in o.deps:
            d.needs = True
        if dma is not None:
            self.dma_count[dma] = self.dma_count.get(dma, 0) + 1
            o.sem, o.val = dma, 16 * self.dma_count[dma]
            o.needs = True
        self.ops[eng].append(o)
        return o

    def fence(self, src_keys, dst_keys):
        users = []
        for k in src_keys:
            w = self.lastw.get(k)
            if w is not None:
                users.append(w)
            users.extend(self.readers.get(k, ()))
        for k in dst_keys:
            self.readers.setdefault(k, []).extend(users)

    def finalize(self):
        for e in self.ENGS:
            n = 0
            for o in self.ops[e]:
                if o.is_dma:
                    continue
                if o.needs:
                    n += 1
                    o.sem, o.val = "c_" + e, n

    def emit(self, eng, engine, sems, final_waits=()):
        waited = {}
        for o in self.ops[eng]:
            need = {}
            for d in o.deps:
                if need.get(d.sem, 0) < d.val:
                    need[d.sem] = d.val
            for s, v in need.items():
                if waited.get(s, 0) < v:
                    engine.wait_ge(sems[s], v)
                    waited[s] = v
            ins = o.fn(engine)
            if o.needs:
                ins.then_inc(sems[o.sem], 16 if o.is_dma else 1)
        for s, v in final_waits:
            engine.wait_ge(sems[s], v)


def build_program(S, NSEQ, DEPTH):
    NCH = S // C
    NTT = S // 128
    off, NC = cpack_layout(DEPTH)
    nc = bass.Bass("TRN2", target_bir_lowering=False)
    dr = {}
    dr["xT"] = nc.dram_tensor("xT", [NSEQ, D, S], F32, kind="ExternalInput")
    dr["cpack"] = nc.dram_tensor("cpack", [128, NC], F32, kind="ExternalInput")
    dr["oh"] = nc.dram_tensor("oh", [32, EW], F32, kind="ExternalInput")
    dr["rbrep"] = nc.dram_tensor("rbrep", [32, 512], F32, kind="ExternalInput")
    dr["w_in"] = nc.dram_tensor("w_in", [DEPTH, D, D_IN], F32, kind="ExternalInput")
    dr["w_out"] = nc.dram_tensor("w_out", [DEPTH, D, D], F32, kind="ExternalInput")
    dr["w_gate"] = nc.dram_tensor("w_gate", [DEPTH, D, D_FF], F32, kind="ExternalInput")
    dr["w_up"] = nc.dram_tensor("w_up", [DEPTH, D, D_FF], F32, kind="ExternalInput")
    dr["w_down"] = nc.dram_tensor("w_down", [DEPTH, D_FF, D], F32, kind="ExternalInput")
    dr["yT"] = nc.dram_tensor("yT", [NSEQ, D, S], F32, kind="ExternalOutput")
    dr["scr"] = nc.dram_tensor("scr", [4, 128 * EW], F32, kind="Internal")

    P = Prog()
    st = ExitStack()
    sb = lambda name, shape: st.enter_context(nc.sbuf_tensor(name, shape, F32))
    xT = sb("xT_sb", [128, KT, C])
    hT = sb("hT_sb", [128, KT, C])
    big = sb("big_sb", [128, 12, C])
    NP = 6
    AW = NT * TW + 2 * 2 * C + NP * C
    arena = sb("arena", [128, AW])
    arena_bf = arena.bitcast(BF16)
    Tt = [arena[:, i * TW:(i + 1) * TW] for i in range(NT)]
    qz = arena[:, NT * TW: NT * TW + 4 * C].rearrange("p (a m t) -> p a m t", a=2, m=2)
    Pt = [arena[:, NT * TW + 4 * C + i * C: NT * TW + 4 * C + (i + 1) * C] for i in range(NP)]
    NWOP = 3
    WH = [arena_bf[:, j * 4096: j * 4096 + 2048] for j in range(NWOP)]
    WL = [arena_bf[:, j * 4096 + 2048: (j + 1) * 4096] for j in range(NWOP)]
    FT = [arena[:, NWOP * 2048 + i * C: NWOP * 2048 + (i + 1) * C] for i in range(6)]
    assert NWOP * 2048 + 6 * C <= AW
    MIX_KEYS = [("T", i) for i in range(NT)] + [("qn", a, m) for a in range(2) for m in range(2)] + \
               [("P", i) for i in range(NP)]
    FFN_KEYS = [("wop", j) for j in range(NWOP)] + [("FT", i) for i in range(6)]
    Kc = [sb(f"Kc{l}", [128, 2, S]) for l in range(DEPTH)]
    VROW = N_HEADS * (HEAD_V + 1)
    Vx = [sb(f"Vx{l}", [128, NTT * VROW + 64]) for l in range(DEPTH)]
    Vx4 = [v[:, 0:NTT * VROW].rearrange("p (t h d) -> p t h d", t=NTT, h=N_HEADS) for v in Vx]
    G = sb("G_sb", [128, N_HEADS, GW])
    hT_bf = hT.bitcast(BF16)
    big_bf = big.bitcast(BF16)
    cp = sb("cp_sb", [128, NC])
    hal = [[sb(f"hal{l}_{i}", [128, HALO]) for i in range(6)] for l in range(DEPTH)]
    ring = [sb(f"ring{i}", [128, 2048]) for i in range(RING)]
    ps = [st.enter_context(nc.psum_tensor(f"ps{i}", [128, C], F32)) for i in range(8)]

    sem_names = ["c_pe", "c_act", "c_dve", "c_pool", "c_sp", "d_setup", "d_scr0", "d_scr1"] + \
                [f"d_ring{i}" for i in range(RING)] + [f"d_x{k}" for k in range(KT)] + [f"d_y{k}" for k in range(KT)]
    sems = {n: st.enter_context(nc.semaphore(n)) for n in sem_names}

    cpc = lambda name, l=None, i=0, w=1: cp[:, off[name if l is None else f"{name}{l}"] + i:
                                              off[name if l is None else f"{name}{l}"] + i + w]
    ones128 = cpc("ones128", w=128)
    bd32 = cpc("bd32", w=128)

    state = {"ring": 0, "ps": 0}

    def ps_next(pool=(0, 1, 2, 3, 4, 5, 6, 7)):
        i = pool[state["ps"] % len(pool)]
        state["ps"] += 1
        return i

    def mm(out, lhsT, rhs, start, stop, reads, writes, tp=None):
        kw = {} if tp is None else {"tile_position": tp}
        return P.op("pe", lambda e: e.matmul(out, lhsT, rhs, start=start, stop=stop, **kw), reads, writes)

    def act(out, in_, func, reads, writes, bias=None, scale=None):
        kw = {}
        if bias is not None:
            kw["bias"] = bias
        if scale is not None:
            kw["scale"] = scale
        return P.op("act", lambda e: e.activation(out, in_, func, **kw), reads, writes)

    def acopy(out, in_, reads, writes):
        return P.op("act", lambda e: e.copy(out, in_), reads, writes)

    def tt(out, in0, in1, op, reads, writes):
        return P.op("dve", lambda e: e.tensor_tensor(out, in0, in1, op), reads, writes)

    def ts(out, in0, s1, s2, op0, op1, reads, writes):
        if op1 is None:
            return P.op("dve", lambda e: e.tensor_scalar(out, in0, s1, None, op0), reads, writes)
        return P.op("dve", lambda e: e.tensor_scalar(out, in0, s1, s2, op0, op1), reads, writes)

    def rsq(out, in_, c, reads, writes):
        P.op("act", lambda e: e.activation(out, in_, AF.Sqrt, bias=c), reads, writes)
        return P.op("dve", lambda e: e.reciprocal(out, out), writes, writes)

    def stt(out, in0, scalar, in1, op0, op1, reads, writes):
        return P.op("dve", lambda e: e.scalar_tensor_tensor(out, in0, scalar, in1, op0, op1), reads, writes)

    def gstt(out, in0, scalar, in1, op0, op1, reads, writes):
        return P.op("pool", lambda e: e.scalar_tensor_tensor(out, in0, scalar, in1, op0, op1), reads, writes)

    def gts(out, in0, s1, s2, op0, op1, reads, writes):
        return P.op("pool", lambda e: e.tensor_scalar(out, in0, s1, s2, op0, op1), reads, writes)

    def vcopy(out, in_, reads, writes):
        return P.op("dve", lambda e: e.tensor_copy(out, in_), reads, writes)

    def vmemset(ap, val, writes):
        return P.op("dve", lambda e: e.memset(ap, val), (), writes)

    def wload(src):
        i = state["ring"] % RING
        state["ring"] += 1
        return i, src

    def ring_dma(dst_fn, src):
        i = state["ring"] % RING
        state["ring"] += 1
        dst = dst_fn(ring[i])
        P.op("sp", lambda e: e.dma_start(out=dst, in_=src), (), [("ring", i)], dma=f"d_ring{i}")
        return i

    r3 = lambda t: t[:, :].rearrange("p (a n) -> p a n", n=256)
    r2 = lambda t: t[:, :].rearrange("p (a n) -> p a n", n=1024)

    P.op("sp", lambda e: e.dma_start(out=cp[:, :], in_=dr["cpack"].ap()), (), ["cp"], dma="d_setup")
    ohs = big[0:32, 0, :]
    ohs2 = big[0:32, 1, 0:256]
    rbs = big[0:32, 2, :]
    ohd = dr["oh"].ap()
    P.op("sp", lambda e: e.dma_start(out=ohs, in_=ohd[:, 0:512]), (), [("big", 0)], dma="d_setup")
    P.op("sp", lambda e: e.dma_start(out=ohs2, in_=ohd[:, 512:768]), (), [("big", 1)], dma="d_setup")
    P.op("sp", lambda e: e.dma_start(out=rbs, in_=dr["rbrep"].ap()), (), [("big", 2)], dma="d_setup")
    n_setup = P.dma_count["d_setup"]
    for o in P.ops["sp"]:
        o.val = 16 * n_setup

    for l in range(DEPTH):
        li = lam_init_of(l)
        ts(cpc("n1g", l, w=8), cpc("n1g", l, w=8), 32.0, None, ALU.mult, None, ["cp"], ["cp"])
        ts(cpc("n2g", l, w=8), cpc("n2g", l, w=8), 32.0, None, ALU.mult, None, ["cp"], ["cp"])
        ts(cpc("clg", l, w=2), cpc("clg", l, w=2), 16.0, None, ALU.mult, None, ["cp"], ["cp"])
        ts(cpc("kg", l), cpc("kg", l), math.sqrt(32.0), None, ALU.mult, None, ["cp"], ["cp"])
        ts(cpc("sg", l), cpc("sg", l), 8.0 * (1.0 - li), None, ALU.mult, None, ["cp"], ["cp"])
        tmp = cpc("ltmp", l, 4, 32)
        s1 = cpc("ltmp", l, 0)
        s2 = cpc("ltmp", l, 1)
        tt(tmp, cpc("lq1", l, w=32), cpc("lk1", l, w=32), ALU.mult, ["cp"], ["cp"])
        P.op("dve", lambda e, s1=s1, tmp=tmp: e.reduce_sum(s1, tmp, mybir.AxisListType.X), ["cp"], ["cp"])
        tt(tmp, cpc("lq2", l, w=32), cpc("lk2", l, w=32), ALU.mult, ["cp"], ["cp"])
        P.op("dve", lambda e, s2=s2, tmp=tmp: e.reduce_sum(s2, tmp, mybir.AxisListType.X), ["cp"], ["cp"])
        act(s1, s1, AF.Exp, ["cp"], ["cp"])
        act(s2, s2, AF.Exp, ["cp"], ["cp"])
        tt(s1, s2, s1, ALU.subtract, ["cp"], ["cp"])
        ts(cpc("neglam", l), s1, -li, None, ALU.add, None, ["cp"], ["cp"])
        vmemset(Vx[l][:, :], 0.0, [("Vx1", l)])
        vmemset(Vx4[l][:, :, :, HEAD_V:HEAD_V + 1], 1.0, [("Vx1", l)])

    for h in range(N_HEADS):
        eb = big[:, 4 + 2 * h: 6 + 2 * h, :].rearrange("p a n -> p (a n)")
        b0, b1 = ps_next(), ps_next()
        lhs = big[0:32, 2, h * 128:(h + 1) * 128]
        mm(ps[b0][:, 0:512], lhs, ohs, True, True, ["cp", ("big", 0), ("big", 2)], [("ps", b0)])
        mm(ps[b1][:, 0:256], lhs, ohs2, True, True, ["cp", ("big", 1), ("big", 2)], [("ps", b1)])
        act(eb[:, 0:512], ps[b0][:, 0:512], AF.Exp, [("ps", b0)], [("eb", h)])
        act(eb[:, 512:768], ps[b1][:, 0:256], AF.Exp, [("ps", b1)], [("eb", h)])
        vmemset(eb[:, 0:127], 0.0, [("eb", h)])
        scr_lin = bass.AP(dr["scr"], h * 128 * EW, [[EW, 128], [1, EW]])
        P.op("sp", lambda e, scr_lin=scr_lin, eb=eb: e.dma_start(out=scr_lin, in_=eb[:, 0:EW]),
             [("eb", h)], [("scr", h)], dma="d_scr0")
        scr_toe = bass.AP(dr["scr"], h * 128 * EW + 127, [[EW - 1, 128], [1, GW]])
        P.op("sp", lambda e, scr_toe=scr_toe, h=h: e.dma_start(out=G[:, h, :], in_=scr_toe),
             [("scr", h)], [("G", h)], dma="d_scr1")
    for o in P.ops["sp"]:
        if o.sem == "d_scr1":
            o.val = 16 * P.dma_count["d_scr1"]
    for h in range(N_HEADS):
        for j in (4 + 2 * h, 5 + 2 * h):
            P.lastw[("big", j)] = P.lastw[("scr", h)]
            P.readers[("big", j)] = []

    def rmsnorm_to_hT(gname, l, split=False, mixer=False):
        if split and mixer:
            tm = [(Tt[i][:, 0:C], ("T", i)) for i in range(5)]
        elif split:
            tm = [(FT[i], ("FT", i)) for i in range(5)]
        else:
            tm = [(Tt[i][:, 0:C], ("T", i)) for i in range(3)]
        b = ps_next()
        for kt in range(KT):
            sq, sk = tm[kt % 2]
            act(sq, xT[:, kt, :], AF.Square, [("xT", kt)], [sk])
            mm(ps[b][:, :], ones128, sq, kt == 0, kt == KT - 1, ["cp", sk], [("ps", b)])
        rs, rk = tm[2]
        rsq(rs, ps[b][:, :], D * EPS, [("ps", b)], [rk])
        for kt in range(KT):
            if not split:
                stt(hT[:, kt, :], xT[:, kt, :], cpc(gname, l, kt), rs, ALU.mult, ALU.mult,
                    ["cp", ("xT", kt), rk], [("hT", kt)])
            else:
                h32, hk = tm[3 + kt % 2]
                stt(h32, xT[:, kt, :], cpc(gname, l, kt), rs, ALU.mult, ALU.mult, ["cp", ("xT", kt), rk], [hk])
                acopy(hT_bf[:, kt, 0:C], h32, [hk], [("hT", kt)])
                tt(hT_bf[:, kt, C:2 * C], h32, hT_bf[:, kt, 0:C], ALU.subtract, [hk, ("hT", kt)], [("hT", kt)])

    def proj_fm(slot, half, bank):
        W = r3(ring[slot])
        for kt in range(KT):
            mm(ps[bank][:, :], W[:, kt, half * 128:(half + 1) * 128], hT[:, kt, :], kt == 0, kt == KT - 1,
               [("ring", slot), ("hT", kt)], [("ps", bank)])

    def win_src(l, gi):
        return dr["w_in"].ap()[l].rearrange("(kt p) n -> p kt n", p=128)[:, :, gi * 256:(gi + 1) * 256]

    def halo_in(l, hi, tile_i, c):
        t = Tt[tile_i]
        if c == 0:
            vmemset(t[:, 0:HALO], 0.0, [("T", tile_i)])
        else:
            vcopy(t[:, 0:HALO], hal[l][hi][:, :], [("hal", l, hi)], [("T", tile_i)])

    def halo_out(l, hi, tile_i):
        vcopy(hal[l][hi][:, :], Tt[tile_i][:, C:C + HALO], [("T", tile_i)], [("hal", l, hi)])

    def mixer_half(l, s, c):
        P.fence(FFN_KEYS, MIX_KEYS)
        P.label = "norm1"
        rmsnorm_to_hT("n1g", l, split=True, mixer=True)
        win3 = dr["w_in"].ap()[l].rearrange("(kt p) n -> p kt n", p=128)
        v3m = lambda t: t.rearrange("p (a n) -> p a n", n=256)
        xs = big_bf[:, 8:12, :].rearrange("p a n -> p (a n)")
        ys = arena_bf[:, 2 * (NT * TW + 4 * C): 2 * (NT * TW + 4 * C) + 4096]
        zs = arena_bf[:, 2 * NT * TW: 2 * NT * TW + 4096]
        SLOTS = {"X": (xs[:, 0:2048], xs[:, 2048:4096], [("big", i) for i in range(8, 12)]),
                 "Y": (ys[:, 0:2048], ys[:, 2048:4096], [("P", i) for i in range(4)]),
                 "Z": (zs[:, 0:2048], zs[:, 2048:4096], [("qn", a, m) for a in range(2) for m in range(2)])}
        sched = "XYZXYZXYX"
        win_ops = {}
        nxt_in = [0]

        def ensure_in(k):
            lab = P.label
            while nxt_in[0] <= min(k, 8):
                gi = nxt_in[0]
                hi_, lo_, keys = SLOTS[sched[gi]]
                r = ring_dma(r3, win3[:, :, gi * 256:(gi + 1) * 256])
                acopy(hi_, ring[r][:, :], [("ring", r)], keys)
                tt(lo_, ring[r][:, :], hi_, ALU.subtract, [("ring", r)] + keys, keys)
                win_ops[gi] = (v3m(hi_), v3m(lo_), keys)
                nxt_in[0] += 1
            P.label = lab

        def stage(gi):
            tgt = gi + 2
            if tgt <= 8 and sched[tgt] == sched[gi]:
                tgt = gi + 1
            ensure_in(tgt)
            return gi

        def proj_fm(gi, half, bank):
            Wp = win_ops[gi][0:2]
            keys = win_ops[gi][2]
            n = 0
            for kt in range(KT):
                for (wi, hi2) in COMBOS:
                    mm(ps[bank][:, :], Wp[wi][:, kt, half * 128:(half + 1) * 128], hT_bf[:, kt, hi2 * C:(hi2 + 1) * C],
                       n == 0, n == 3 * KT - 1, keys + [("hT", kt)], [("ps", bank)])
                    n += 1

        ensure_in(1)
        P.label = "pool"
        U, PA, PB = 3, 4, 5
        PLs = (6, 2)
        slot = stage(0)
        for hf in range(2):
            b = ps_next()
            proj_fm(slot, hf, b)
            halo_in(l, hf, U, c)
            acopy(Tt[U][:, HALO:TW], ps[b][:, :], [("ps", b)], [("T", U)])
            halo_out(l, hf, U)
            u = Tt[U]
            a_, b_ = Tt[PA], Tt[PB]
            tt(a_[:, 1:TW], u[:, 1:TW], u[:, 0:TW - 1], ALU.add, [("T", U)], [("T", PA)])
            tt(b_[:, 3:TW], a_[:, 3:TW], a_[:, 1:TW - 2], ALU.add, [("T", PA)], [("T", PB)])
            if hf == 0:
                wlo, whi = 2.0, 4.0
            else:
                tt(a_[:, 7:TW], b_[:, 7:TW], b_[:, 3:TW - 4], ALU.add, [("T", PB)], [("T", PA)])
                tt(b_[:, 15:TW], a_[:, 15:TW], a_[:, 7:TW - 8], ALU.add, [("T", PA)], [("T", PB)])
                wlo, whi = 8.0, 16.0
            lo, hi_ = a_, b_
            PL = PLs[hf]
            pl = Tt[PL]
            stt(pl[0:64, 0:C], lo[0:64, HALO:TW], 1.0 / wlo, u[0:64, HALO:TW], ALU.mult, ALU.subtract,
                [("T", PA), ("T", PB), ("T", U)], [("T", PL)])
            stt(pl[64:128, 0:C], hi_[64:128, HALO:TW], 1.0 / whi, u[64:128, HALO:TW], ALU.mult, ALU.subtract,
                [("T", PA), ("T", PB), ("T", U)], [("T", PL)])
            if c == 0:
                ic = cp[:, off["invc"] + hf * 16: off["invc"] + hf * 16 + 16]
                for (r0, r1, src) in ((0, 64, lo), (64, 128, hi_)):
                    tt(pl[r0:r1, 0:16], src[r0:r1, HALO:HALO + 16], ic[r0:r1, :], ALU.mult,
                       ["cp", ("T", PA), ("T", PB)], [("T", PL)])
                    tt(pl[r0:r1, 0:16], pl[r0:r1, 0:16], u[r0:r1, HALO:HALO + 16], ALU.subtract,
                       [("T", U), ("T", PL)], [("T", PL)])
        P.label = "sconv"
        SB, SC, V, TA = (0, 1), (4, 5), 3, 6
        slot = stage(1)
        for hf in range(2):
            b = ps_next()
            proj_fm(slot, hf, b)
            acopy(Tt[SB[hf]][:, 0:C], ps[b][:, :], [("ps", b)], [("T", SB[hf])])
        P.label = "sconv"
        slot = stage(2)
        for hf in range(2):
            b = ps_next()
            proj_fm(slot, hf, b)
            acopy(Tt[SC[hf]][:, 0:C], ps[b][:, :], [("ps", b)], [("T", SC[hf])])
        P.label = "pool"
        for hf in range(2):
            b2 = ps_next()
            mm(ps[b2][:, :], cpc("bd", l, hf * 128, 128), Tt[PLs[hf]][:, 0:C], True, True,
               ["cp", ("T", PLs[hf])], [("ps", b2)])
            act(big[:, hf, :], ps[b2][:, :], AF.Identity, ["cp", ("ps", b2)], [("big", hf)],
                scale=cpc("pscale", l, hf))
        P.label = "sconv"
        slot = stage(3)
        for hf in range(2):
            b = ps_next()
            proj_fm(slot, hf, b)
            halo_in(l, 2 + hf, V, c)
            v = Tt[V]
            tt(v[:, HALO:TW], Tt[SC[hf]][:, 0:C], ps[b][:, :], ALU.mult, [("T", SC[hf]), ("ps", b)], [("T", V)])
            halo_out(l, 2 + hf, V)
            ta = Tt[TA]
            w = lambda j: cpc("sconv", l, hf * 3 + j)
            ts(ta[:, 0:C], v[:, HALO - 2:TW - 2], w(0), None, ALU.mult, None, ["cp", ("T", V)], [("T", TA)])
            stt(ta[:, 0:C], v[:, HALO - 1:TW - 1], w(1), ta[:, 0:C], ALU.mult, ALU.add,
                ["cp", ("T", V), ("T", TA)], [("T", TA)])
            stt(ta[:, 0:C], v[:, HALO:TW], w(2), ta[:, 0:C], ALU.mult, ALU.add,
                ["cp", ("T", V), ("T", TA)], [("T", TA)])
            tt(big[:, 2 + hf, :], ta[:, 0:C], Tt[SB[hf]][:, 0:C], ALU.mult,
               [("T", TA), ("T", SB[hf])], [("big", 2 + hf)])
        P.label = "conf"
        CA, SIG, CUs, ACC = (0, 1), 4, (3, 2), (5, 6)
        conv_taps = []
        slot = stage(4)
        for hf in range(2):
            b = ps_next()
            proj_fm(slot, hf, b)
            acopy(Tt[CA[hf]][:, 0:C], ps[b][:, :], [("ps", b)], [("T", CA[hf])])
        slot = stage(5)
        for hf in range(2):
            b = ps_next()
            proj_fm(slot, hf, b)
            act(Tt[SIG][:, 0:C], ps[b][:, :], AF.Sigmoid, [("ps", b)], [("T", SIG)])
            CU = CUs[hf]
            halo_in(l, 4 + hf, CU, c)
            cu = Tt[CU]
            tt(cu[:, HALO:TW], Tt[SIG][:, 0:C], Tt[CA[hf]][:, 0:C], ALU.mult,
               [("T", SIG), ("T", CA[hf])], [("T", CU)])
            halo_out(l, 4 + hf, CU)
            acc = Tt[ACC[hf]]

            def tap(j, hf=hf, cu=cu, acc=acc, CU=CU):
                src = cu[:, HALO - (CONF_K - 1) + j: HALO - (CONF_K - 1) + j + C]
                wj = cpc("cdw", l, hf * 31 + j)
                if j == 0:
                    ts(acc[:, 0:C], src, wj, cpc("cdb", l, hf), ALU.mult, ALU.add,
                       ["cp", ("T", CU)], [("T", ACC[hf])])
                else:
                    stt(acc[:, 0:C], src, wj, acc[:, 0:C], ALU.mult, ALU.add,
                        ["cp", ("T", CU), ("T", ACC[hf])], [("T", ACC[hf])])
            conv_taps.extend([(lambda j=j, tap=tap: tap(j)) for j in range(CONF_K)])
        def emit_taps(n):
            lab = P.label
            P.label = "conf"
            for _ in range(n):
                if conv_taps:
                    conv_taps.pop(0)()
            P.label = lab

        emit_taps(16)
        P.label = "qk"
        SQs, Rr = (4, 7), 4
        for gi, gname in ((6, "q"), (7, "k")):
            slot = stage(gi)
            banks = []
            for hf in range(2):
                b = ps_next()
                proj_fm(slot, hf, b)
                act(Tt[SQs[hf]][:, 0:C], ps[b][:, :], AF.Square, [("ps", b)], [("T", SQs[hf])])
                banks.append(b)
            for hf in range(2):
                b = banks[hf]
                SQ = SQs[hf]
                b2 = ps_next()
                mm(ps[b2][:, :], bd32, Tt[SQ][:, 0:C], True, True, ["cp", ("T", SQ)], [("ps", b2)])
                rsq(Tt[SQ][:, 0:C], ps[b2][:, :], 32.0 * EPS, [("ps", b2)], [("T", SQ)])
                if gi == 6:
                    for m in range(2):
                        stt(qz[:, hf, m, :], ps[b][:, :], cpc(f"qg{m}", l), Tt[SQ][:, 0:C], ALU.mult, ALU.mult,
                            ["cp", ("ps", b), ("T", SQ)], [("qn", hf, m)])
                else:
                    stt(Kc[l][:, hf, c * C:(c + 1) * C], ps[b][:, :], cpc("kg", l), Tt[SQ][:, 0:C],
                        ALU.mult, ALU.mult, ["cp", ("ps", b), ("T", SQ)], [("Kc", l, hf, c)])
            emit_taps(8 if gi == 6 else 30)
        slot = stage(8)
        Wv = win_ops[8][0:2]
        vkeys = win_ops[8][2]

        def vproj(tti):
            P.label = "v"
            b = ps_next()
            n = 0
            for kt in range(KT):
                for (hp, wp) in COMBOS:
                    mm(ps[b][:, 0:256], hT_bf[:, kt, hp * C + tti * 128: hp * C + (tti + 1) * 128], Wv[wp][:, kt, :],
                       n == 0, n == 3 * KT - 1, vkeys + [("hT", kt)], [("ps", b)])
                    n += 1
            idx = c * (C // 128) + tti
            acopy(Vx4[l][:, idx, :, 0:HEAD_V], ps[b][:, 0:256].rearrange("p (h d) -> p h d", h=N_HEADS),
                  [("ps", b)], [("Vx", l, idx)])

        for tti in range(C // 128):
            vproj(tti)
        P.label = "attn"
        OS, RR = (0, 1), (4, 7)
        bO = (4, 5)
        nk = 4 * c + 4
        steps = [(h, kt) for h in range(N_HEADS) for kt in range(nk)]

        def geom(kt):
            j0 = max(0, kt - 4 * c) * 128
            return j0, C - j0

        def qk(i):
            P.label = f"attn_qk_c{c}"
            h, kt = steps[i]
            qt, hb = h // 2, (h % 2) * 64
            j0, n = geom(kt)
            far = kt <= 4 * c - 2
            delta = C * c - 128 * kt
            for m in range(2):
                bS = ps_next((0, 1, 2, 3))
                pi = (2 * i + m) % NP
                mm(ps[bS][:, 0:n], Kc[l][hb:hb + 64, qt, kt * 128:(kt + 1) * 128], qz[hb:hb + 64, qt, m, j0:C],
                   True, True, [("Kc", l, qt, kt // 4), ("qn", qt, m)], [("ps", bS)])
                if far:
                    act(Pt[pi][:, 0:n], ps[bS][:, 0:n], AF.Exp, ["cp", ("ps", bS)], [("P", pi)],
                        bias=cpc("rb31", None, h))
                else:
                    act(Pt[pi][:, 0:n], ps[bS][:, 0:n], AF.Exp, [("ps", bS)], [("P", pi)])
                    g0 = delta + j0
                    tt(Pt[pi][:, 0:n], Pt[pi][:, 0:n], G[:, h, g0:g0 + n], ALU.mult,
                       [("P", pi), ("G", h)], [("P", pi)])

        def av(i):
            P.label = f"attn_av_c{c}"
            h, kt = steps[i]
            j0, n = geom(kt)
            base = kt * VROW + (h * (HEAD_V + 1) if h % 2 == 0 else (h - 1) * (HEAD_V + 1) + 1)
            for m in range(2):
                pi = (2 * i + m) % NP
                mm(ps[bO[m]][:, j0:C], Vx[l][:, base:base + 128], Pt[pi][:, 0:n], kt == 0, kt == nk - 1,
                   [("Vx", l, kt), ("Vx1", l), ("P", pi)], [("ps", bO[m])])

        def tail_a1(h):
            for m in range(2):
                acopy(Tt[OS[m]][:, 0:C], ps[bO[m]][:, :], [("ps", bO[m])], [("T", OS[m])])

        def rows(h):
            return (0, 64) if h % 2 == 0 else (64, 128)

        def tail_a2(h):
            P.label = f"attn_tailA_c{c}"
            r0, r1 = rows(h)
            for m in range(2):
                bD = ps_next((6, 7))
                mm(ps[bD][:, :], cpc("sel2", w=128), Tt[OS[m]][:, 0:C], True, True, ["cp", ("T", OS[m])], [("ps", bD)])
                P.op("dve", lambda e, o=Tt[RR[m]][r0:r1, 0:C], i=ps[bD][r0:r1, :]: e.reciprocal(o, i),
                     [("ps", bD)], [("T", RR[m])])
                tt(Tt[RR[m]][r0:r1, 0:C], Tt[OS[m]][r0:r1, 0:C], Tt[RR[m]][r0:r1, 0:C], ALU.mult,
                   [("T", OS[m]), ("T", RR[m])], [("T", RR[m])])
            stt(Tt[RR[0]][r0:r1, 0:C], Tt[RR[1]][r0:r1, 0:C], cp[r0:r1, off[f"neglam{l}"]:off[f"neglam{l}"] + 1],
                Tt[RR[0]][r0:r1, 0:C], ALU.mult, ALU.add, ["cp", ("T", RR[1]), ("T", RR[0])], [("T", RR[0])])
            act(Tt[OS[0]][r0:r1, 0:C], Tt[RR[0]][r0:r1, 0:C], AF.Square, [("T", RR[0])], [("T", OS[0])])

        def tail_b(h):
            P.label = f"attn_tailB_c{c}"
            r0, r1 = rows(h)
            bq = ps_next((6, 7))
            mm(ps[bq][:, :], cpc("bd64", w=128), Tt[OS[0]][:, 0:C], True, True, ["cp", ("T", OS[0])], [("ps", bq)])
            rsq(Tt[OS[1]][r0:r1, 0:C], ps[bq][r0:r1, :], 64.0 * EPS, [("ps", bq)], [("T", OS[1])])
            stt(big[r0:r1, 6 + h // 2, :], Tt[RR[0]][r0:r1, 0:C], cp[r0:r1, off[f"sg{l}"]:off[f"sg{l}"] + 1],
                Tt[OS[1]][r0:r1, 0:C], ALU.mult, ALU.mult, ["cp", ("T", RR[0]), ("T", OS[1])],
                [("big", 6 + h // 2)])

        def run_pending(pending, upto):
            keep = []
            for (when, fn, hh) in pending:
                if upto is None or when <= upto:
                    fn(hh)
                else:
                    keep.append((when, fn, hh))
            return keep

        SQc = CUs

        def confln_part1():
            lab = P.label
            emit_taps(2 * CONF_K)
            P.label = "confln"
            bm = ps_next((6, 7))
            for hf in range(2):
                mm(ps[bm][:, :], ones128, Tt[ACC[hf]][:, 0:C], hf == 0, hf == 1, ["cp", ("T", ACC[hf])],
                   [("ps", bm)])
            for hf in range(2):
                stt(Tt[ACC[hf]][:, 0:C], ps[bm][:, :], -1.0 / 256.0, Tt[ACC[hf]][:, 0:C], ALU.mult, ALU.add,
                    [("ps", bm), ("T", ACC[hf])], [("T", ACC[hf])])
                act(Tt[SQc[hf]][:, 0:C], Tt[ACC[hf]][:, 0:C], AF.Square, [("T", ACC[hf])], [("T", SQc[hf])])
            P.label = lab

        def confln_part2():
            lab = P.label
            P.label = "confln"
            bv = ps_next((6, 7))
            for hf in range(2):
                mm(ps[bv][:, :], ones128, Tt[SQc[hf]][:, 0:C], hf == 0, hf == 1, ["cp", ("T", SQc[hf])],
                   [("ps", bv)])
            RSTD = SQc[0]
            rsq(Tt[RSTD][:, 0:C], ps[bv][:, :], 256.0 * EPS, [("ps", bv)], [("T", RSTD)])
            for hf in range(2):
                tt(Tt[ACC[hf]][:, 0:C], Tt[ACC[hf]][:, 0:C], Tt[RSTD][:, 0:C], ALU.mult,
                   [("T", ACC[hf]), ("T", RSTD)], [("T", ACC[hf])])
                act(big[:, 4 + hf, :], Tt[ACC[hf]][:, 0:C], AF.Silu, ["cp", ("T", ACC[hf])], [("big", 4 + hf)],
                    bias=cpc("clb", l, hf), scale=cpc("clg", l, hf))
            P.label = lab

        pending = []
        TAPS_PER_STEP = 1
        qk(0)
        qk(1)
        for i in range(len(steps)):
            if i + 2 < len(steps):
                qk(i + 2)
            emit_taps(TAPS_PER_STEP)
            av(i)
            if i == 10:
                confln_part1()
            if i == 13:
                confln_part2()
            pending = run_pending(pending, i)
            h, kt = steps[i]
            if kt == nk - 1:
                tail_a1(h)
                pending.append((i + 1, tail_a2, h))
                pending.append((i + min(nk, 7), tail_b, h))
        pending = run_pending(pending, None)
        P.label = "wout"
        wo = dr["w_out"].ap()[l]
        nK = 8
        ki = 0
        for g in range(4):
            slot = ring_dma(r2, wo[g * 256:(g + 1) * 256, :].rearrange("(a p) n -> p a n", p=128))
            W = r2(ring[slot])
            for a in range(2):
                for m in range(KT):
                    mm(ps[m][:, :], W[:, a, m * 128:(m + 1) * 128], big[:, 2 * g + a, :], ki == 0, ki == nK - 1,
                       [("ring", slot), ("big", 2 * g + a)], [("ps", m)])
                ki += 1
        for m in range(KT):
            tt(xT[:, m, :], ps[m][:, :], xT[:, m, :], ALU.add, [("ps", m), ("xT", m)], [("xT", m)])

    wop_state = {"n": 0}

    def wsplit(rslot):
        j = wop_state["n"] % NWOP
        wop_state["n"] += 1
        acopy(WH[j], ring[rslot][:, :], [("ring", rslot)], [("wop", j)])
        tt(WL[j], ring[rslot][:, :], WH[j], ALU.subtract, [("ring", rslot), ("wop", j)], [("wop", j)])
        return j

    COMBOS = ((0, 0), (0, 1), (1, 0))

    def ffn_half(l):
        P.fence(MIX_KEYS, FFN_KEYS)
        P.label = "norm2"
        rmsnorm_to_hT("n2g", l, split=True)
        wg = dr["w_gate"].ap()[l].rearrange("(kt p) n -> p kt n", p=128)
        wu = dr["w_up"].ap()[l].rearrange("(kt p) n -> p kt n", p=128)
        wd = dr["w_down"].ap()[l]
        v3 = lambda t: t.rearrange("p (a n) -> p a n", n=256)
        v2 = lambda t: t.rearrange("p (a n) -> p a n", n=1024)

        def projb(j, t, bank):
            Wp = (v3(WH[j]), v3(WL[j]))
            n = 0
            for kt in range(KT):
                for (wi, hi_) in COMBOS:
                    mm(ps[bank][:, :], Wp[wi][:, kt, t * 128:(t + 1) * 128], hT_bf[:, kt, hi_ * C:(hi_ + 1) * C],
                       n == 0, n == 3 * KT - 1, [("wop", j), ("hT", kt)], [("ps", bank)])
                    n += 1

        halves = (range(0, 6), range(6, NG_FF))
        seq = []
        for groups in halves:
            for g in groups:
                seq += [("g", g), ("u", g)]
            for g in groups:
                seq += [("d", g)]
        issued = {}
        nxt = [0]

        def ensure(k):
            lab = P.label
            while nxt[0] <= k and nxt[0] < len(seq):
                kind, g = seq[nxt[0]]
                P.label = "down" if kind == "d" else "gateup"
                if kind == "g":
                    r = ring_dma(r3, wg[:, :, g * 256:(g + 1) * 256])
                elif kind == "u":
                    r = ring_dma(r3, wu[:, :, g * 256:(g + 1) * 256])
                else:
                    r = ring_dma(r2, wd[g * 256:(g + 1) * 256, :].rearrange("(a p) n -> p a n", p=128))
                issued[nxt[0]] = wsplit(r)
                nxt[0] += 1
            P.label = lab

        ensure(1)
        idx = 0
        for groups in halves:
            g0 = groups[0]
            P.label = "gateup"
            for g in groups:
                jg = issued[idx]
                banks_g = []
                for t in range(2):
                    bg = ps_next()
                    projb(jg, t, bg)
                    banks_g.append(bg)
                ensure(idx + 2)
                for t in range(2):
                    act(FT[t], ps[banks_g[t]][:, :], AF.Silu, [("ps", banks_g[t])], [("FT", t)])
                ju = issued[idx + 1]
                banks_u = []
                for t in range(2):
                    bu = ps_next()
                    projb(ju, t, bu)
                    banks_u.append(bu)
                ensure(idx + 3)
                for t in range(2):
                    ai = (g - g0) * 2 + t
                    bu = banks_u[t]
                    tt(FT[2 + t], FT[t], ps[bu][:, :], ALU.mult, [("FT", t), ("ps", bu)], [("FT", 2 + t)])
                    acopy(big_bf[:, ai, 0:C], FT[2 + t], [("FT", 2 + t)], [("big", ai)])
                    tt(big_bf[:, ai, C:2 * C], FT[2 + t], big_bf[:, ai, 0:C], ALU.subtract,
                       [("FT", 2 + t), ("big", ai)], [("big", ai)])
                idx += 2
            P.label = "down"
            nK = 2 * len(groups) * 3
            ki = 0
            for g in groups:
                j = issued[idx]
                Wp = (v2(WH[j]), v2(WL[j]))
                for a in range(2):
                    ai = (g - g0) * 2 + a
                    for (wi, xi) in COMBOS:
                        for m in range(KT):
                            mm(ps[m][:, :], Wp[wi][:, a, m * 128:(m + 1) * 128], big_bf[:, ai, xi * C:(xi + 1) * C],
                               ki == 0, ki == nK - 1, [("wop", j), ("big", ai)], [("ps", m)])
                        ki += 1
                ensure(idx + 2)
                idx += 1
            for m in range(KT):
                tt(xT[:, m, :], ps[m][:, :], xT[:, m, :], ALU.add, [("ps", m), ("xT", m)], [("xT", m)])

    xin = dr["xT"].ap()
    yout = dr["yT"].ap()
    chunks = [(s_, c_) for s_ in range(NSEQ) for c_ in range(NCH)]

    def xload(s_, c_, kt):
        src = xin[s_][kt * 128:(kt + 1) * 128, c_ * C:(c_ + 1) * C]
        P.op("pool", lambda e: e.dma_start(out=xT[:, kt, :], in_=src), (), [("xT", kt)], dma=f"d_x{kt}")

    def ystore(s_, c_, kt):
        dst = yout[s_][kt * 128:(kt + 1) * 128, c_ * C:(c_ + 1) * C]
        P.op("pool", lambda e: e.dma_start(out=dst, in_=xT[:, kt, :]), [("xT", kt)], (), dma=f"d_y{kt}")

    P.label = "io"
    for kt in range(KT):
        xload(*chunks[0], kt)
    for ci, (s, c) in enumerate(chunks):
        for l in range(DEPTH):
            mixer_half(l, s, c)
            ffn_half(l)
        P.label = "io"
        for kt in range(KT):
            ystore(s, c, kt)
        if ci + 1 < len(chunks):
            for kt in range(KT):
                xload(*chunks[ci + 1], kt)

    P.finalize()
    build_program.last_prog = P
    with nc.Block() as block:
        @block.tensor
        def _(e):
            P.emit("pe", e, sems)

        @block.scalar
        def _(e):
            P.emit("act", e, sems)

        @block.vector
        def _(e):
            P.emit("dve", e, sems)

        @block.gpsimd
        def _(e):
            P.emit("pool", e, sems, final_waits=[(f"d_y{k}", 16 * P.dma_count[f"d_y{k}"]) for k in range(KT)])

        @block.sync
        def _(e):
            P.emit("sp", e, sems)
    st.close()
    return nc


def run(inputs, S, NSEQ, DEPTH, n_cores=N_CORES, trace=False):
    x = np.asarray(inputs["x"], np.float32)
    inp = {k: np.asarray(v, np.float32) for k, v in inputs.items()}
    cpack = build_cpack(inp, DEPTH)
    oh, rbrep = build_bias_tables(inp)
    nc = build_program(S, NSEQ, DEPTH)
    in_maps = []
    for i in range(n_cores):
        xs = np.ascontiguousarray(x[i * NSEQ:(i + 1) * NSEQ].transpose(0, 2, 1))
        in_maps.append({
            "xT": xs, "cpack": cpack, "oh": oh, "rbrep": rbrep,
            "w_in": np.ascontiguousarray(inp["w_in"][:DEPTH]), "w_out": np.ascontiguousarray(inp["w_out"][:DEPTH]),
            "w_gate": np.ascontiguousarray(inp["w_gate"][:DEPTH]), "w_up": np.ascontiguousarray(inp["w_up"][:DEPTH]),
            "w_down": np.ascontiguousarray(inp["w_down"][:DEPTH]),
        })
    res = run_bass_kernel_spmd(nc, in_maps, core_ids=list(range(n_cores)), trace=trace)
    out = np.concatenate([np.asarray(r["yT"]).transpose(0, 2, 1) for r in res.results], axis=0)
    return np.ascontiguousarray(out.astype(np.float32)), res


def kernel(**inputs):
    out, _ = run(inputs, S=2048, NSEQ=2, DEPTH=2)
    return out
```

```python
import math
from contextlib import ExitStack

import numpy as np
import concourse.bass as bass
import concourse.mybir as mybir
from concourse.bass_utils import run_bass_kernel_spmd

F32 = mybir.dt.float32
BF16 = mybir.dt.bfloat16
AF = mybir.ActivationFunctionType
ALU = mybir.AluOpType

D = 1024
KT = D // 128
D_IN = 2304
D_FF = 2816
NG_FF = D_FF // 256
N_HEADS = 4
HEAD_V = 64
HEAD_QK = 32
CONF_K = 31
EPS = 1e-6
C = 512
HALO = 32
TW = C + HALO
GW = 640
EW = 768
FAR_MIN = 113
N_CORES = 8
RING = 4
NT = 8


def lam_init_of(l):
    return 0.8 - 0.6 * math.exp(-0.3 * l)


def cpack_layout(depth):
    off = {}
    n = 0

    def add(name, w):
        nonlocal n
        off[name] = n
        n += w

    add("ones128", 128)
    add("bd32", 128)
    add("sel2", 128)
    add("bd64", 128)
    add("rb31", 4)
    add("invc", 32)
    for l in range(depth):
        for name, w in (("n1g", 8), ("n2g", 8), ("pscale", 2), ("sconv", 6), ("cdw", 62), ("cdb", 2),
                        ("clg", 2), ("clb", 2), ("qg0", 1), ("qg1", 1), ("kg", 1), ("sg", 1), ("lq1", 32), ("lk1", 32),
                        ("lq2", 32), ("lk2", 32), ("bd", 256), ("neglam", 1), ("ltmp", 36)):
            add(f"{name}{l}", w)
    return off, n


def rel_bucket_np(n):
    n = np.asarray(n, dtype=np.int64)
    nf = np.maximum(n, 1).astype(np.float32)
    large = 16 + (np.log(nf / np.float32(16)) / np.float32(math.log(128 / 16)) * np.float32(16)).astype(np.int32)
    large = np.minimum(large, 31)
    return np.where(n < 16, n, large)


def build_cpack(inp, depth):
    off, n = cpack_layout(depth)
    cp = np.zeros((128, n), np.float32)
    p = np.arange(128)
    cp[:, off["ones128"]:off["ones128"] + 128] = 1.0
    bd32 = (p[:, None] // 32 == p[None, :] // 32).astype(np.float32)
    cp[:, off["bd32"]:off["bd32"] + 128] = bd32
    cp[64, off["sel2"]:off["sel2"] + 64] = 1.0
    cp[63, off["sel2"] + 64:off["sel2"] + 128] = 1.0
    cp[:, off["bd64"]:off["bd64"] + 128] = (p[:, None] // 64 == p[None, :] // 64).astype(np.float32)
    cp[:, off["rb31"]:off["rb31"] + 4] = inp["rel_bias"][31][None, :]
    t = np.arange(16)
    for hf in range(2):
        win = np.where(p < 64, 2 ** (2 * hf + 1), 2 ** (2 * hf + 2))
        cp[:, off["invc"] + hf * 16: off["invc"] + hf * 16 + 16] = \
            1.0 / np.minimum(t[None, :] + 1, win[:, None]).astype(np.float32)
    for l in range(depth):
        o = lambda nm: off[f"{nm}{l}"]
        cp[:, o("n1g"):o("n1g") + 8] = inp["norm1_g"][l].reshape(8, 128).T
        cp[:, o("n2g"):o("n2g") + 8] = inp["norm2_g"][l].reshape(8, 128).T
        cp[:, o("pscale"):o("pscale") + 2] = inp["pool_scale"][l].reshape(2, 128).T
        cp[:, o("sconv"):o("sconv") + 6] = inp["sconv_w"][l].reshape(3, 2, 128).transpose(2, 1, 0).reshape(128, 6)
        cp[:, o("cdw"):o("cdw") + 62] = inp["conf_dw_w"][l].reshape(31, 2, 128).transpose(2, 1, 0).reshape(128, 62)
        cp[:, o("cdb"):o("cdb") + 2] = inp["conf_dw_b"][l].reshape(2, 128).T
        cp[:, o("clg"):o("clg") + 2] = inp["conf_ln_g"][l].reshape(2, 128).T
        cp[:, o("clb"):o("clb") + 2] = inp["conf_ln_b"][l].reshape(2, 128).T
        cp[:, o("qg0")] = np.where((p // 32) % 2 == 0, inp["q_norm_g"][l][p % 32], 0.0)
        cp[:, o("qg1")] = np.where((p // 32) % 2 == 1, inp["q_norm_g"][l][p % 32], 0.0)
        cp[:, o("kg")] = inp["k_norm_g"][l][p % 32]
        cp[:, o("sg")] = inp["subln_g"][l][p % 64]
        lamv = {"lq1": inp["lam_q1"], "lk1": inp["lam_k1"], "lq2": inp["lam_q2"], "lk2": inp["lam_k2"]}
        for nm in ("lq1", "lk1", "lq2", "lk2"):
            cp[:, o(nm):o(nm) + 32] = lamv[nm][l][None, :]
        for hf in range(2):
            for g in range(2):
                cp[g * 64:(g + 1) * 64, o("bd") + hf * 128 + g * 64: o("bd") + hf * 128 + (g + 1) * 64] = \
                    inp["pool_w"][l][2 * hf + g]
    return cp


def build_bias_tables(inp):
    oh = np.zeros((32, EW), np.float32)
    d = np.arange(EW - 127)
    b = rel_bucket_np(d)
    oh[b, d + 127] = 1.0
    rbrep = np.repeat(inp["rel_bias"].astype(np.float32)[:, :, None], 128, axis=2).reshape(32, 4 * 128)
    return oh, rbrep


class Op:
    __slots__ = ("eng", "fn", "deps", "needs", "sem", "val", "is_dma", "label")


class Prog:
    ENGS = ("pe", "act", "dve", "pool", "sp")

    def __init__(self):
        self.ops = {e: [] for e in self.ENGS}
        self.lastw = {}
        self.readers = {}
        self.dma_count = {}
        self.label = "setup"

    def op(self, eng, fn, reads=(), writes=(), dma=None):
        o = Op()
        o.label = self.label
        o.eng, o.fn, o.deps, o.needs, o.sem, o.val = eng, fn, set(), False, None, None
        o.is_dma = dma is not None
        for k in reads:
            w = self.lastw.get(k)
            if w is not None:
                o.deps.add(w)
        for k in writes:
            w = self.lastw.get(k)
            if w is not None:
                o.deps.add(w)
            for r in self.readers.get(k, ()):
                o.deps.add(r)
        if eng == "pe":
            o.deps = {d for d in o.deps if d.eng != "pe" or d.is_dma}
        for k in reads:
            self.readers.setdefault(k, []).append(o)
        for k in writes:
            self.lastw[k] = o
            self.readers[k] = []
        for d in o.deps:
            d.needs = True
        if dma is not None:
            self.dma_count[dma] = self.dma_count.get(dma, 0) + 1
            o.sem, o.val = dma, 16 * self.dma_count[dma]
            o.needs = True
        self.ops[eng].append(o)
        return o

    def fence(self, src_keys, dst_keys):
        users = []
        for k in src_keys:
            w = self.lastw.get(k)
            if w is not None:
                users.append(w)
            users.extend(self.readers.get(k, ()))
        for k in dst_keys:
            self.readers.setdefault(k, []).extend(users)

    def finalize(self):
        for e in self.ENGS:
            n = 0
            for o in self.ops[e]:
                if o.is_dma:
                    continue
                if o.needs:
                    n += 1
                    o.sem, o.val = "c_" + e, n

    def emit(self, eng, engine, sems, final_waits=()):
        waited = {}
        for o in self.ops[eng]:
            need = {}
            for d in o.deps:
                if need.get(d.sem, 0) < d.val:
                    need[d.sem] = d.val
            for s, v in need.items():
                if waited.get(s, 0) < v:
                    engine.wait_ge(sems[s], v)
                    waited[s] = v
            ins = o.fn(engine)
            if o.needs:
                ins.then_inc(sems[o.sem], 16 if o.is_dma else 1)
        for s, v in final_waits:
            engine.wait_ge(sems[s], v)


def build_program(S, NSEQ, DEPTH):
    NCH = S // C
    NTT = S // 128
    off, NC = cpack_layout(DEPTH)
    nc = bass.Bass("TRN2", target_bir_lowering=False)
    dr = {}
    dr["xT"] = nc.dram_tensor("xT", [NSEQ, D, S], F32, kind="ExternalInput")
    dr["cpack"] = nc.dram_tensor("cpack", [128, NC], F32, kind="ExternalInput")
    dr["oh"] = nc.dram_tensor("oh", [32, EW], F32, kind="ExternalInput")
    dr["rbrep"] = nc.dram_tensor("rbrep", [32, 512], F32, kind="ExternalInput")
    dr["w_in"] = nc.dram_tensor("w_in", [DEPTH, D, D_IN], F32, kind="ExternalInput")
    dr["w_out"] = nc.dram_tensor("w_out", [DEPTH, D, D], F32, kind="ExternalInput")
    dr["w_gate"] = nc.dram_tensor("w_gate", [DEPTH, D, D_FF], F32, kind="ExternalInput")
    dr["w_up"] = nc.dram_tensor("w_up", [DEPTH, D, D_FF], F32, kind="ExternalInput")
    dr["w_down"] = nc.dram_tensor("w_down", [DEPTH, D_FF, D], F32, kind="ExternalInput")
    dr["yT"] = nc.dram_tensor("yT", [NSEQ, D, S], F32, kind="ExternalOutput")
    dr["scr"] = nc.dram_tensor("scr", [4, 128 * EW], F32, kind="Internal")

    P = Prog()
    st = ExitStack()
    sb = lambda name, shape: st.enter_context(nc.sbuf_tensor(name, shape, F32))
    xT = sb("xT_sb", [128, KT, C])
    hT = sb("hT_sb", [128, KT, C])
    big = sb("big_sb", [128, 12, C])
    NP = 6
    AW = NT * TW + 2 * 2 * C + NP * C
    arena = sb("arena", [128, AW])
    arena_bf = arena.bitcast(BF16)
    Tt = [arena[:, i * TW:(i + 1) * TW] for i in range(NT)]
    qz = arena[:, NT * TW: NT * TW + 4 * C].rearrange("p (a m t) -> p a m t", a=2, m=2)
    Pt = [arena[:, NT * TW + 4 * C + i * C: NT * TW + 4 * C + (i + 1) * C] for i in range(NP)]
    NWOP = 3
    WH = [arena_bf[:, j * 4096: j * 4096 + 2048] for j in range(NWOP)]
    WL = [arena_bf[:, j * 4096 + 2048: (j + 1) * 4096] for j in range(NWOP)]
    FT = [arena[:, NWOP * 2048 + i * C: NWOP * 2048 + (i + 1) * C] for i in range(6)]
    assert NWOP * 2048 + 6 * C <= AW
    MIX_KEYS = [("T", i) for i in range(NT)] + [("qn", a, m) for a in range(2) for m in range(2)] + \
               [("P", i) for i in range(NP)]
    FFN_KEYS = [("wop", j) for j in range(NWOP)] + [("FT", i) for i in range(6)]
    Kc = [sb(f"Kc{l}", [128, 2, S]) for l in range(DEPTH)]
    VROW = N_HEADS * (HEAD_V + 1)
    Vx = [sb(f"Vx{l}", [128, NTT * VROW + 64]) for l in range(DEPTH)]
    Vx4 = [v[:, 0:NTT * VROW].rearrange("p (t h d) -> p t h d", t=NTT, h=N_HEADS) for v in Vx]
    G = sb("G_sb", [128, N_HEADS, GW])
    hT_bf = hT.bitcast(BF16)
    big_bf = big.bitcast(BF16)
    cp = sb("cp_sb", [128, NC])
    hal = [[sb(f"hal{l}_{i}", [128, HALO]) for i in range(6)] for l in range(DEPTH)]
    ring = [sb(f"ring{i}", [128, 2048]) for i in range(RING)]
    ps = [st.enter_context(nc.psum_tensor(f"ps{i}", [128, C], F32)) for i in range(8)]

    sem_names = ["c_pe", "c_act", "c_dve", "c_pool", "c_sp", "d_setup", "d_scr0", "d_scr1"] + \
                [f"d_ring{i}" for i in range(RING)] + [f"d_x{k}" for k in range(KT)] + [f"d_y{k}" for k in range(KT)]
    sems = {n: st.enter_context(nc.semaphore(n)) for n in sem_names}

    cpc = lambda name, l=None, i=0, w=1: cp[:, off[name if l is None else f"{name}{l}"] + i:
                                              off[name if l is None else f"{name}{l}"] + i + w]
    ones128 = cpc("ones128", w=128)
    bd32 = cpc("bd32", w=128)

    state = {"ring": 0, "ps": 0}

    def ps_next(pool=(0, 1, 2, 3, 4, 5, 6, 7)):
        i = pool[state["ps"] % len(pool)]
        state["ps"] += 1
        return i

    def mm(out, lhsT, rhs, start, stop, reads, writes, tp=None):
        kw = {} if tp is None else {"tile_position": tp}
        return P.op("pe", lambda e: e.matmul(out, lhsT, rhs, start=start, stop=stop, **kw), reads, writes)

    def act(out, in_, func, reads, writes, bias=None, scale=None):
        kw = {}
        if bias is not None:
            kw["bias"] = bias
        if scale is not None:
            kw["scale"] = scale
        return P.op("act", lambda e: e.activation(out, in_, func, **kw), reads, writes)

    def acopy(out, in_, reads, writes):
        return P.op("act", lambda e: e.copy(out, in_), reads, writes)

    def tt(out, in0, in1, op, reads, writes):
        return P.op("dve", lambda e: e.tensor_tensor(out, in0, in1, op), reads, writes)

    def ts(out, in0, s1, s2, op0, op1, reads, writes):
        if op1 is None:
            return P.op("dve", lambda e: e.tensor_scalar(out, in0, s1, None, op0), reads, writes)
        return P.op("dve", lambda e: e.tensor_scalar(out, in0, s1, s2, op0, op1), reads, writes)

    def rsq(out, in_, c, reads, writes):
        P.op("act", lambda e: e.activation(out, in_, AF.Sqrt, bias=c), reads, writes)
        return P.op("dve", lambda e: e.reciprocal(out, out), writes, writes)

    def stt(out, in0, scalar, in1, op0, op1, reads, writes):
        return P.op("dve", lambda e: e.scalar_tensor_tensor(out, in0, scalar, in1, op0, op1), reads, writes)

    def gstt(out, in0, scalar, in1, op0, op1, reads, writes):
        return P.op("pool", lambda e: e.scalar_tensor_tensor(out, in0, scalar, in1, op0, op1), reads, writes)

    def gts(out, in0, s1, s2, op0, op1, reads, writes):
        return P.op("pool", lambda e: e.tensor_scalar(out, in0, s1, s2, op0, op1), reads, writes)

    def vcopy(out, in_, reads, writes):
        return P.op("dve", lambda e: e.tensor_copy(out, in_), reads, writes)

    def vmemset(ap, val, writes):
        return P.op("dve", lambda e: e.memset(ap, val), (), writes)

    def wload(src):
        i = state["ring"] % RING
        state["ring"] += 1
        return i, src

    def ring_dma(dst_fn, src):
        i = state["ring"] % RING
        state["ring"] += 1
        dst = dst_fn(ring[i])
        P.op("sp", lambda e: e.dma_start(out=dst, in_=src), (), [("ring", i)], dma=f"d_ring{i}")
        return i

    r3 = lambda t: t[:, :].rearrange("p (a n) -> p a n", n=256)
    r2 = lambda t: t[:, :].rearrange("p (a n) -> p a n", n=1024)

    P.op("sp", lambda e: e.dma_start(out=cp[:, :], in_=dr["cpack"].ap()), (), ["cp"], dma="d_setup")
    ohs = big[0:32, 0, :]
    ohs2 = big[0:32, 1, 0:256]
    rbs = big[0:32, 2, :]
    ohd = dr["oh"].ap()
    P.op("sp", lambda e: e.dma_start(out=ohs, in_=ohd[:, 0:512]), (), [("big", 0)], dma="d_setup")
    P.op("sp", lambda e: e.dma_start(out=ohs2, in_=ohd[:, 512:768]), (), [("big", 1)], dma="d_setup")
    P.op("sp", lambda e: e.dma_start(out=rbs, in_=dr["rbrep"].ap()), (), [("big", 2)], dma="d_setup")
    n_setup = P.dma_count["d_setup"]
    for o in P.ops["sp"]:
        o.val = 16 * n_setup

    for l in range(DEPTH):
        li = lam_init_of(l)
        ts(cpc("n1g", l, w=8), cpc("n1g", l, w=8), 32.0, None, ALU.mult, None, ["cp"], ["cp"])
        ts(cpc("n2g", l, w=8), cpc("n2g", l, w=8), 32.0, None, ALU.mult, None, ["cp"], ["cp"])
        ts(cpc("clg", l, w=2), cpc("clg", l, w=2), 16.0, None, ALU.mult, None, ["cp"], ["cp"])
        ts(cpc("kg", l), cpc("kg", l), math.sqrt(32.0), None, ALU.mult, None, ["cp"], ["cp"])
        ts(cpc("sg", l), cpc("sg", l), 8.0 * (1.0 - li), None, ALU.mult, None, ["cp"], ["cp"])
        tmp = cpc("ltmp", l, 4, 32)
        s1 = cpc("ltmp", l, 0)
        s2 = cpc("ltmp", l, 1)
        tt(tmp, cpc("lq1", l, w=32), cpc("lk1", l, w=32), ALU.mult, ["cp"], ["cp"])
        P.op("dve", lambda e, s1=s1, tmp=tmp: e.reduce_sum(s1, tmp, mybir.AxisListType.X), ["cp"], ["cp"])
        tt(tmp, cpc("lq2", l, w=32), cpc("lk2", l, w=32), ALU.mult, ["cp"], ["cp"])
        P.op("dve", lambda e, s2=s2, tmp=tmp: e.reduce_sum(s2, tmp, mybir.AxisListType.X), ["cp"], ["cp"])
        act(s1, s1, AF.Exp, ["cp"], ["cp"])
        act(s2, s2, AF.Exp, ["cp"], ["cp"])
        tt(s1, s2, s1, ALU.subtract, ["cp"], ["cp"])
        ts(cpc("neglam", l), s1, -li, None, ALU.add, None, ["cp"], ["cp"])
        vmemset(Vx[l][:, :], 0.0, [("Vx1", l)])
        vmemset(Vx4[l][:, :, :, HEAD_V:HEAD_V + 1], 1.0, [("Vx1", l)])

    for h in range(N_HEADS):
        eb = big[:, 4 + 2 * h: 6 + 2 * h, :].rearrange("p a n -> p (a n)")
        b0, b1 = ps_next(), ps_next()
        lhs = big[0:32, 2, h * 128:(h + 1) * 128]
        mm(ps[b0][:, 0:512], lhs, ohs, True, True, ["cp", ("big", 0), ("big", 2)], [("ps", b0)])
        mm(ps[b1][:, 0:256], lhs, ohs2, True, True, ["cp", ("big", 1), ("big", 2)], [("ps", b1)])
        act(eb[:, 0:512], ps[b0][:, 0:512], AF.Exp, [("ps", b0)], [("eb", h)])
        act(eb[:, 512:768], ps[b1][:, 0:256], AF.Exp, [("ps", b1)], [("eb", h)])
        vmemset(eb[:, 0:127], 0.0, [("eb", h)])
        scr_lin = bass.AP(dr["scr"], h * 128 * EW, [[EW, 128], [1, EW]])
        P.op("sp", lambda e, scr_lin=scr_lin, eb=eb: e.dma_start(out=scr_lin, in_=eb[:, 0:EW]),
             [("eb", h)], [("scr", h)], dma="d_scr0")
        scr_toe = bass.AP(dr["scr"], h * 128 * EW + 127, [[EW - 1, 128], [1, GW]])
        P.op("sp", lambda e, scr_toe=scr_toe, h=h: e.dma_start(out=G[:, h, :], in_=scr_toe),
             [("scr", h)], [("G", h)], dma="d_scr1")
    for o in P.ops["sp"]:
        if o.sem == "d_scr1":
            o.val = 16 * P.dma_count["d_scr1"]
    for h in range(N_HEADS):
        for j in (4 + 2 * h, 5 + 2 * h):
            P.lastw[("big", j)] = P.lastw[("scr", h)]
            P.readers[("big", j)] = []

    def rmsnorm_to_hT(gname, l, split=False):
        if split:
            tm = [(FT[i], ("FT", i)) for i in range(5)]
        else:
            tm = [(Tt[i][:, 0:C], ("T", i)) for i in range(3)]
        b = ps_next()
        for kt in range(KT):
            sq, sk = tm[kt % 2]
            act(sq, xT[:, kt, :], AF.Square, [("xT", kt)], [sk])
            mm(ps[b][:, :], ones128, sq, kt == 0, kt == KT - 1, ["cp", sk], [("ps", b)])
        rs, rk = tm[2]
        rsq(rs, ps[b][:, :], D * EPS, [("ps", b)], [rk])
        for kt in range(KT):
            if not split:
                stt(hT[:, kt, :], xT[:, kt, :], cpc(gname, l, kt), rs, ALU.mult, ALU.mult,
                    ["cp", ("xT", kt), rk], [("hT", kt)])
            else:
                h32, hk = tm[3 + kt % 2]
                stt(h32, xT[:, kt, :], cpc(gname, l, kt), rs, ALU.mult, ALU.mult, ["cp", ("xT", kt), rk], [hk])
                acopy(hT_bf[:, kt, 0:C], h32, [hk], [("hT", kt)])
                tt(hT_bf[:, kt, C:2 * C], h32, hT_bf[:, kt, 0:C], ALU.subtract, [hk, ("hT", kt)], [("hT", kt)])

    def proj_fm(slot, half, bank):
        W = r3(ring[slot])
        for kt in range(KT):
            mm(ps[bank][:, :], W[:, kt, half * 128:(half + 1) * 128], hT[:, kt, :], kt == 0, kt == KT - 1,
               [("ring", slot), ("hT", kt)], [("ps", bank)])

    def win_src(l, gi):
        return dr["w_in"].ap()[l].rearrange("(kt p) n -> p kt n", p=128)[:, :, gi * 256:(gi + 1) * 256]

    def halo_in(l, hi, tile_i, c):
        t = Tt[tile_i]
        if c == 0:
            vmemset(t[:, 0:HALO], 0.0, [("T", tile_i)])
        else:
            vcopy(t[:, 0:HALO], hal[l][hi][:, :], [("hal", l, hi)], [("T", tile_i)])

    def halo_out(l, hi, tile_i):
        vcopy(hal[l][hi][:, :], Tt[tile_i][:, C:C + HALO], [("T", tile_i)], [("hal", l, hi)])

    def mixer_half(l, s, c):
        P.fence(FFN_KEYS, MIX_KEYS)
        P.label = "norm1"
        rmsnorm_to_hT("n1g", l)
        P.label = "pool"
        U, PA, PB = 3, 4, 5
        PLs = (6, 2)
        slot = ring_dma(r3, win_src(l, 0))
        for hf in range(2):
            b = ps_next()
            proj_fm(slot, hf, b)
            halo_in(l, hf, U, c)
            acopy(Tt[U][:, HALO:TW], ps[b][:, :], [("ps", b)], [("T", U)])
            halo_out(l, hf, U)
            u = Tt[U]
            a_, b_ = Tt[PA], Tt[PB]
            tt(a_[:, 1:TW], u[:, 1:TW], u[:, 0:TW - 1], ALU.add, [("T", U)], [("T", PA)])
            tt(b_[:, 3:TW], a_[:, 3:TW], a_[:, 1:TW - 2], ALU.add, [("T", PA)], [("T", PB)])
            if hf == 0:
                wlo, whi = 2.0, 4.0
            else:
                tt(a_[:, 7:TW], b_[:, 7:TW], b_[:, 3:TW - 4], ALU.add, [("T", PB)], [("T", PA)])
                tt(b_[:, 15:TW], a_[:, 15:TW], a_[:, 7:TW - 8], ALU.add, [("T", PA)], [("T", PB)])
                wlo, whi = 8.0, 16.0
            lo, hi_ = a_, b_
            PL = PLs[hf]
            pl = Tt[PL]
            stt(pl[0:64, 0:C], lo[0:64, HALO:TW], 1.0 / wlo, u[0:64, HALO:TW], ALU.mult, ALU.subtract,
                [("T", PA), ("T", PB), ("T", U)], [("T", PL)])
            stt(pl[64:128, 0:C], hi_[64:128, HALO:TW], 1.0 / whi, u[64:128, HALO:TW], ALU.mult, ALU.subtract,
                [("T", PA), ("T", PB), ("T", U)], [("T", PL)])
            if c == 0:
                ic = cp[:, off["invc"] + hf * 16: off["invc"] + hf * 16 + 16]
                for (r0, r1, src) in ((0, 64, lo), (64, 128, hi_)):
                    tt(pl[r0:r1, 0:16], src[r0:r1, HALO:HALO + 16], ic[r0:r1, :], ALU.mult,
                       ["cp", ("T", PA), ("T", PB)], [("T", PL)])
                    tt(pl[r0:r1, 0:16], pl[r0:r1, 0:16], u[r0:r1, HALO:HALO + 16], ALU.subtract,
                       [("T", U), ("T", PL)], [("T", PL)])
        P.label = "sconv"
        SB, SC, V, TA = (0, 1), (4, 5), 3, 6
        slot = ring_dma(r3, win_src(l, 1))
        for hf in range(2):
            b = ps_next()
            proj_fm(slot, hf, b)
            acopy(Tt[SB[hf]][:, 0:C], ps[b][:, :], [("ps", b)], [("T", SB[hf])])
        P.label = "sconv"
        slot = ring_dma(r3, win_src(l, 2))
        for hf in range(2):
            b = ps_next()
            proj_fm(slot, hf, b)
            acopy(Tt[SC[hf]][:, 0:C], ps[b][:, :], [("ps", b)], [("T", SC[hf])])
        P.label = "pool"
        for hf in range(2):
            b2 = ps_next()
            mm(ps[b2][:, :], cpc("bd", l, hf * 128, 128), Tt[PLs[hf]][:, 0:C], True, True,
               ["cp", ("T", PLs[hf])], [("ps", b2)])
            act(big[:, hf, :], ps[b2][:, :], AF.Identity, ["cp", ("ps", b2)], [("big", hf)],
                scale=cpc("pscale", l, hf))
        P.label = "sconv"
        slot = ring_dma(r3, win_src(l, 3))
        for hf in range(2):
            b = ps_next()
            proj_fm(slot, hf, b)
            halo_in(l, 2 + hf, V, c)
            v = Tt[V]
            tt(v[:, HALO:TW], Tt[SC[hf]][:, 0:C], ps[b][:, :], ALU.mult, [("T", SC[hf]), ("ps", b)], [("T", V)])
            halo_out(l, 2 + hf, V)
            ta = Tt[TA]
            w = lambda j: cpc("sconv", l, hf * 3 + j)
            ts(ta[:, 0:C], v[:, HALO - 2:TW - 2], w(0), None, ALU.mult, None, ["cp", ("T", V)], [("T", TA)])
            stt(ta[:, 0:C], v[:, HALO - 1:TW - 1], w(1), ta[:, 0:C], ALU.mult, ALU.add,
                ["cp", ("T", V), ("T", TA)], [("T", TA)])
            stt(ta[:, 0:C], v[:, HALO:TW], w(2), ta[:, 0:C], ALU.mult, ALU.add,
                ["cp", ("T", V), ("T", TA)], [("T", TA)])
            tt(big[:, 2 + hf, :], ta[:, 0:C], Tt[SB[hf]][:, 0:C], ALU.mult,
               [("T", TA), ("T", SB[hf])], [("big", 2 + hf)])
        P.label = "conf"
        CA, SIG, CUs, ACC = (0, 1), 4, (3, 2), (5, 6)
        conv_taps = []
        slot = ring_dma(r3, win_src(l, 4))
        for hf in range(2):
            b = ps_next()
            proj_fm(slot, hf, b)
            acopy(Tt[CA[hf]][:, 0:C], ps[b][:, :], [("ps", b)], [("T", CA[hf])])
        slot = ring_dma(r3, win_src(l, 5))
        for hf in range(2):
            b = ps_next()
            proj_fm(slot, hf, b)
            act(Tt[SIG][:, 0:C], ps[b][:, :], AF.Sigmoid, [("ps", b)], [("T", SIG)])
            CU = CUs[hf]
            halo_in(l, 4 + hf, CU, c)
            cu = Tt[CU]
            tt(cu[:, HALO:TW], Tt[SIG][:, 0:C], Tt[CA[hf]][:, 0:C], ALU.mult,
               [("T", SIG), ("T", CA[hf])], [("T", CU)])
            halo_out(l, 4 + hf, CU)
            acc = Tt[ACC[hf]]

            def tap(j, hf=hf, cu=cu, acc=acc, CU=CU):
                src = cu[:, HALO - (CONF_K - 1) + j: HALO - (CONF_K - 1) + j + C]
                wj = cpc("cdw", l, hf * 31 + j)
                if j == 0:
                    ts(acc[:, 0:C], src, wj, cpc("cdb", l, hf), ALU.mult, ALU.add,
                       ["cp", ("T", CU)], [("T", ACC[hf])])
                else:
                    stt(acc[:, 0:C], src, wj, acc[:, 0:C], ALU.mult, ALU.add,
                        ["cp", ("T", CU), ("T", ACC[hf])], [("T", ACC[hf])])
            conv_taps.extend([(lambda j=j, tap=tap: tap(j)) for j in range(CONF_K)])
        def emit_taps(n):
            lab = P.label
            P.label = "conf"
            for _ in range(n):
                if conv_taps:
                    conv_taps.pop(0)()
            P.label = lab

        emit_taps(16)
        P.label = "qk"
        SQs, Rr = (4, 7), 4
        for gi, gname in ((6, "q"), (7, "k")):
            slot = ring_dma(r3, win_src(l, gi))
            banks = []
            for hf in range(2):
                b = ps_next()
                proj_fm(slot, hf, b)
                act(Tt[SQs[hf]][:, 0:C], ps[b][:, :], AF.Square, [("ps", b)], [("T", SQs[hf])])
                banks.append(b)
            for hf in range(2):
                b = banks[hf]
                SQ = SQs[hf]
                b2 = ps_next()
                mm(ps[b2][:, :], bd32, Tt[SQ][:, 0:C], True, True, ["cp", ("T", SQ)], [("ps", b2)])
                rsq(Tt[SQ][:, 0:C], ps[b2][:, :], 32.0 * EPS, [("ps", b2)], [("T", SQ)])
                if gi == 6:
                    for m in range(2):
                        stt(qz[:, hf, m, :], ps[b][:, :], cpc(f"qg{m}", l), Tt[SQ][:, 0:C], ALU.mult, ALU.mult,
                            ["cp", ("ps", b), ("T", SQ)], [("qn", hf, m)])
                else:
                    stt(Kc[l][:, hf, c * C:(c + 1) * C], ps[b][:, :], cpc("kg", l), Tt[SQ][:, 0:C],
                        ALU.mult, ALU.mult, ["cp", ("ps", b), ("T", SQ)], [("Kc", l, hf, c)])
            emit_taps(8 if gi == 6 else 30)
        slot = ring_dma(r3, win_src(l, 8))
        W = r3(ring[slot])

        def vproj(tti):
            P.label = "v"
            b = ps_next()
            for kt in range(KT):
                mm(ps[b][:, 0:256], hT[:, kt, tti * 128:(tti + 1) * 128], W[:, kt, :], kt == 0, kt == KT - 1,
                   [("ring", slot), ("hT", kt)], [("ps", b)])
            idx = c * (C // 128) + tti
            acopy(Vx4[l][:, idx, :, 0:HEAD_V], ps[b][:, 0:256].rearrange("p (h d) -> p h d", h=N_HEADS),
                  [("ps", b)], [("Vx", l, idx)])

        for tti in range(C // 128):
            vproj(tti)
        P.label = "attn"
        OS, RR = (0, 1), (4, 7)
        bO = (4, 5)
        nk = 4 * c + 4
        steps = [(h, kt) for h in range(N_HEADS) for kt in range(nk)]

        def geom(kt):
            j0 = max(0, kt - 4 * c) * 128
            return j0, C - j0

        def qk(i):
            P.label = f"attn_qk_c{c}"
            h, kt = steps[i]
            qt, hb = h // 2, (h % 2) * 64
            j0, n = geom(kt)
            far = kt <= 4 * c - 2
            delta = C * c - 128 * kt
            for m in range(2):
                bS = ps_next((0, 1, 2, 3))
                pi = (2 * i + m) % NP
                mm(ps[bS][:, 0:n], Kc[l][hb:hb + 64, qt, kt * 128:(kt + 1) * 128], qz[hb:hb + 64, qt, m, j0:C],
                   True, True, [("Kc", l, qt, kt // 4), ("qn", qt, m)], [("ps", bS)])
                if far:
                    act(Pt[pi][:, 0:n], ps[bS][:, 0:n], AF.Exp, ["cp", ("ps", bS)], [("P", pi)],
                        bias=cpc("rb31", None, h))
                else:
                    act(Pt[pi][:, 0:n], ps[bS][:, 0:n], AF.Exp, [("ps", bS)], [("P", pi)])
                    g0 = delta + j0
                    tt(Pt[pi][:, 0:n], Pt[pi][:, 0:n], G[:, h, g0:g0 + n], ALU.mult,
                       [("P", pi), ("G", h)], [("P", pi)])

        def av(i):
            P.label = f"attn_av_c{c}"
            h, kt = steps[i]
            j0, n = geom(kt)
            base = kt * VROW + (h * (HEAD_V + 1) if h % 2 == 0 else (h - 1) * (HEAD_V + 1) + 1)
            for m in range(2):
                pi = (2 * i + m) % NP
                mm(ps[bO[m]][:, j0:C], Vx[l][:, base:base + 128], Pt[pi][:, 0:n], kt == 0, kt == nk - 1,
                   [("Vx", l, kt), ("Vx1", l), ("P", pi)], [("ps", bO[m])])

        def tail_a1(h):
            for m in range(2):
                acopy(Tt[OS[m]][:, 0:C], ps[bO[m]][:, :], [("ps", bO[m])], [("T", OS[m])])

        def rows(h):
            return (0, 64) if h % 2 == 0 else (64, 128)

        def tail_a2(h):
            P.label = f"attn_tailA_c{c}"
            r0, r1 = rows(h)
            for m in range(2):
                bD = ps_next((6, 7))
                mm(ps[bD][:, :], cpc("sel2", w=128), Tt[OS[m]][:, 0:C], True, True, ["cp", ("T", OS[m])], [("ps", bD)])
                P.op("dve", lambda e, o=Tt[RR[m]][r0:r1, 0:C], i=ps[bD][r0:r1, :]: e.reciprocal(o, i),
                     [("ps", bD)], [("T", RR[m])])
                tt(Tt[RR[m]][r0:r1, 0:C], Tt[OS[m]][r0:r1, 0:C], Tt[RR[m]][r0:r1, 0:C], ALU.mult,
                   [("T", OS[m]), ("T", RR[m])], [("T", RR[m])])
            stt(Tt[RR[0]][r0:r1, 0:C], Tt[RR[1]][r0:r1, 0:C], cp[r0:r1, off[f"neglam{l}"]:off[f"neglam{l}"] + 1],
                Tt[RR[0]][r0:r1, 0:C], ALU.mult, ALU.add, ["cp", ("T", RR[1]), ("T", RR[0])], [("T", RR[0])])
            act(Tt[OS[0]][r0:r1, 0:C], Tt[RR[0]][r0:r1, 0:C], AF.Square, [("T", RR[0])], [("T", OS[0])])

        def tail_b(h):
            P.label = f"attn_tailB_c{c}"
            r0, r1 = rows(h)
            bq = ps_next((6, 7))
            mm(ps[bq][:, :], cpc("bd64", w=128), Tt[OS[0]][:, 0:C], True, True, ["cp", ("T", OS[0])], [("ps", bq)])
            rsq(Tt[OS[1]][r0:r1, 0:C], ps[bq][r0:r1, :], 64.0 * EPS, [("ps", bq)], [("T", OS[1])])
            stt(big[r0:r1, 6 + h // 2, :], Tt[RR[0]][r0:r1, 0:C], cp[r0:r1, off[f"sg{l}"]:off[f"sg{l}"] + 1],
                Tt[OS[1]][r0:r1, 0:C], ALU.mult, ALU.mult, ["cp", ("T", RR[0]), ("T", OS[1])],
                [("big", 6 + h // 2)])

        def run_pending(pending, upto):
            keep = []
            for (when, fn, hh) in pending:
                if upto is None or when <= upto:
                    fn(hh)
                else:
                    keep.append((when, fn, hh))
            return keep

        SQc = CUs

        def confln_part1():
            lab = P.label
            emit_taps(2 * CONF_K)
            P.label = "confln"
            bm = ps_next((6, 7))
            for hf in range(2):
                mm(ps[bm][:, :], ones128, Tt[ACC[hf]][:, 0:C], hf == 0, hf == 1, ["cp", ("T", ACC[hf])],
                   [("ps", bm)])
            for hf in range(2):
                stt(Tt[ACC[hf]][:, 0:C], ps[bm][:, :], -1.0 / 256.0, Tt[ACC[hf]][:, 0:C], ALU.mult, ALU.add,
                    [("ps", bm), ("T", ACC[hf])], [("T", ACC[hf])])
                act(Tt[SQc[hf]][:, 0:C], Tt[ACC[hf]][:, 0:C], AF.Square, [("T", ACC[hf])], [("T", SQc[hf])])
            P.label = lab

        def confln_part2():
            lab = P.label
            P.label = "confln"
            bv = ps_next((6, 7))
            for hf in range(2):
                mm(ps[bv][:, :], ones128, Tt[SQc[hf]][:, 0:C], hf == 0, hf == 1, ["cp", ("T", SQc[hf])],
                   [("ps", bv)])
            RSTD = SQc[0]
            rsq(Tt[RSTD][:, 0:C], ps[bv][:, :], 256.0 * EPS, [("ps", bv)], [("T", RSTD)])
            for hf in range(2):
                tt(Tt[ACC[hf]][:, 0:C], Tt[ACC[hf]][:, 0:C], Tt[RSTD][:, 0:C], ALU.mult,
                   [("T", ACC[hf]), ("T", RSTD)], [("T", ACC[hf])])
                act(big[:, 4 + hf, :], Tt[ACC[hf]][:, 0:C], AF.Silu, ["cp", ("T", ACC[hf])], [("big", 4 + hf)],
                    bias=cpc("clb", l, hf), scale=cpc("clg", l, hf))
            P.label = lab

        pending = []
        TAPS_PER_STEP = 1
        qk(0)
        qk(1)
        for i in range(len(steps)):
            if i + 2 < len(steps):
                qk(i + 2)
            emit_taps(TAPS_PER_STEP)
            av(i)
            if i == 10:
                confln_part1()
            if i == 13:
                confln_part2()
            pending = run_pending(pending, i)
            h, kt = steps[i]
            if kt == nk - 1:
                tail_a1(h)
                pending.append((i + 1, tail_a2, h))
                pending.append((i + min(nk, 7), tail_b, h))
        pending = run_pending(pending, None)
        P.label = "wout"
        wo = dr["w_out"].ap()[l]
        nK = 8
        ki = 0
        for g in range(4):
            slot = ring_dma(r2, wo[g * 256:(g + 1) * 256, :].rearrange("(a p) n -> p a n", p=128))
            W = r2(ring[slot])
            for a in range(2):
                for m in range(KT):
                    mm(ps[m][:, :], W[:, a, m * 128:(m + 1) * 128], big[:, 2 * g + a, :], ki == 0, ki == nK - 1,
                       [("ring", slot), ("big", 2 * g + a)], [("ps", m)])
                ki += 1
        for m in range(KT):
            tt(xT[:, m, :], ps[m][:, :], xT[:, m, :], ALU.add, [("ps", m), ("xT", m)], [("xT", m)])

    wop_state = {"n": 0}

    def wsplit(rslot):
        j = wop_state["n"] % NWOP
        wop_state["n"] += 1
        acopy(WH[j], ring[rslot][:, :], [("ring", rslot)], [("wop", j)])
        tt(WL[j], ring[rslot][:, :], WH[j], ALU.subtract, [("ring", rslot), ("wop", j)], [("wop", j)])
        return j

    COMBOS = ((0, 0), (0, 1), (1, 0))

    def ffn_half(l):
        P.fence(MIX_KEYS, FFN_KEYS)
        P.label = "norm2"
        rmsnorm_to_hT("n2g", l, split=True)
        wg = dr["w_gate"].ap()[l].rearrange("(kt p) n -> p kt n", p=128)
        wu = dr["w_up"].ap()[l].rearrange("(kt p) n -> p kt n", p=128)
        wd = dr["w_down"].ap()[l]
        v3 = lambda t: t.rearrange("p (a n) -> p a n", n=256)
        v2 = lambda t: t.rearrange("p (a n) -> p a n", n=1024)

        def projb(j, t, bank):
            Wp = (v3(WH[j]), v3(WL[j]))
            n = 0
            for kt in range(KT):
                for (wi, hi_) in COMBOS:
                    mm(ps[bank][:, :], Wp[wi][:, kt, t * 128:(t + 1) * 128], hT_bf[:, kt, hi_ * C:(hi_ + 1) * C],
                       n == 0, n == 3 * KT - 1, [("wop", j), ("hT", kt)], [("ps", bank)])
                    n += 1

        halves = (range(0, 6), range(6, NG_FF))
        seq = []
        for groups in halves:
            for g in groups:
                seq += [("g", g), ("u", g)]
            for g in groups:
                seq += [("d", g)]
        issued = {}
        nxt = [0]

        def ensure(k):
            lab = P.label
            while nxt[0] <= k and nxt[0] < len(seq):
                kind, g = seq[nxt[0]]
                P.label = "down" if kind == "d" else "gateup"
                if kind == "g":
                    r = ring_dma(r3, wg[:, :, g * 256:(g + 1) * 256])
                elif kind == "u":
                    r = ring_dma(r3, wu[:, :, g * 256:(g + 1) * 256])
                else:
                    r = ring_dma(r2, wd[g * 256:(g + 1) * 256, :].rearrange("(a p) n -> p a n", p=128))
                issued[nxt[0]] = wsplit(r)
                nxt[0] += 1
            P.label = lab

        ensure(1)
        idx = 0
        for groups in halves:
            g0 = groups[0]
            P.label = "gateup"
            for g in groups:
                jg = issued[idx]
                banks_g = []
                for t in range(2):
                    bg = ps_next()
                    projb(jg, t, bg)
                    banks_g.append(bg)
                ensure(idx + 2)
                for t in range(2):
                    act(FT[t], ps[banks_g[t]][:, :], AF.Silu, [("ps", banks_g[t])], [("FT", t)])
                ju = issued[idx + 1]
                banks_u = []
                for t in range(2):
                    bu = ps_next()
                    projb(ju, t, bu)
                    banks_u.append(bu)
                ensure(idx + 3)
                for t in range(2):
                    ai = (g - g0) * 2 + t
                    bu = banks_u[t]
                    tt(FT[2 + t], FT[t], ps[bu][:, :], ALU.mult, [("FT", t), ("ps", bu)], [("FT", 2 + t)])
                    acopy(big_bf[:, ai, 0:C], FT[2 + t], [("FT", 2 + t)], [("big", ai)])
                    tt(big_bf[:, ai, C:2 * C], FT[2 + t], big_bf[:, ai, 0:C], ALU.subtract,
                       [("FT", 2 + t), ("big", ai)], [("big", ai)])
                idx += 2
            P.label = "down"
            nK = 2 * len(groups) * 3
            ki = 0
            base = state["ps"] % 8
            bank_of = [(base + m) % 8 for m in range(KT)]
            for g in groups:
                j = issued[idx]
                Wp = (v2(WH[j]), v2(WL[j]))
                for a in range(2):
                    ai = (g - g0) * 2 + a
                    for (wi, xi) in COMBOS:
                        for m in range(KT):
                            bk = bank_of[m]
                            mm(ps[bk][:, :], Wp[wi][:, a, m * 128:(m + 1) * 128], big_bf[:, ai, xi * C:(xi + 1) * C],
                               ki == 0, ki == nK - 1, [("wop", j), ("big", ai)], [("ps", bk)])
                        ki += 1
                ensure(idx + 2)
                idx += 1
            for m in range(KT):
                bk = bank_of[m]
                tt(xT[:, m, :], ps[bk][:, :], xT[:, m, :], ALU.add, [("ps", bk), ("xT", m)], [("xT", m)])

    xin = dr["xT"].ap()
    yout = dr["yT"].ap()
    chunks = [(s_, c_) for s_ in range(NSEQ) for c_ in range(NCH)]

    def xload(s_, c_, kt):
        src = xin[s_][kt * 128:(kt + 1) * 128, c_ * C:(c_ + 1) * C]
        P.op("pool", lambda e: e.dma_start(out=xT[:, kt, :], in_=src), (), [("xT", kt)], dma=f"d_x{kt}")

    def ystore(s_, c_, kt):
        dst = yout[s_][kt * 128:(kt + 1) * 128, c_ * C:(c_ + 1) * C]
        P.op("pool", lambda e: e.dma_start(out=dst, in_=xT[:, kt, :]), [("xT", kt)], (), dma=f"d_y{kt}")

    P.label = "io"
    for kt in range(KT):
        xload(*chunks[0], kt)
    for ci, (s, c) in enumerate(chunks):
        for l in range(DEPTH):
            mixer_half(l, s, c)
            ffn_half(l)
        P.label = "io"
        for kt in range(KT):
            ystore(s, c, kt)
        if ci + 1 < len(chunks):
            for kt in range(KT):
                xload(*chunks[ci + 1], kt)

    P.finalize()
    build_program.last_prog = P
    with nc.Block() as block:
        @block.tensor
        def _(e):
            P.emit("pe", e, sems)

        @block.scalar
        def _(e):
            P.emit("act", e, sems)

        @block.vector
        def _(e):
            P.emit("dve", e, sems)

        @block.gpsimd
        def _(e):
            P.emit("pool", e, sems, final_waits=[(f"d_y{k}", 16 * P.dma_count[f"d_y{k}"]) for k in range(KT)])

        @block.sync
        def _(e):
            P.emit("sp", e, sems)
    st.close()
    return nc


def run(inputs, S, NSEQ, DEPTH, n_cores=N_CORES, trace=False):
    x = np.asarray(inputs["x"], np.float32)
    inp = {k: np.asarray(v, np.float32) for k, v in inputs.items()}
    cpack = build_cpack(inp, DEPTH)
    oh, rbrep = build_bias_tables(inp)
    nc = build_program(S, NSEQ, DEPTH)
    in_maps = []
    for i in range(n_cores):
        xs = np.ascontiguousarray(x[i * NSEQ:(i + 1) * NSEQ].transpose(0, 2, 1))
        in_maps.append({
            "xT": xs, "cpack": cpack, "oh": oh, "rbrep": rbrep,
            "w_in": np.ascontiguousarray(inp["w_in"][:DEPTH]), "w_out": np.ascontiguousarray(inp["w_out"][:DEPTH]),
            "w_gate": np.ascontiguousarray(inp["w_gate"][:DEPTH]), "w_up": np.ascontiguousarray(inp["w_up"][:DEPTH]),
            "w_down": np.ascontiguousarray(inp["w_down"][:DEPTH]),
        })
    res = run_bass_kernel_spmd(nc, in_maps, core_ids=list(range(n_cores)), trace=trace)
    out = np.concatenate([np.asarray(r["yT"]).transpose(0, 2, 1) for r in res.results], axis=0)
    return np.ascontiguousarray(out.astype(np.float32)), res


def kernel(**inputs):
    out, _ = run(inputs, S=2048, NSEQ=2, DEPTH=2)
    return out
```

```python
import math
from contextlib import ExitStack

import numpy as np
import concourse.bass as bass
import concourse.mybir as mybir
from concourse.bass_utils import run_bass_kernel_spmd

F32 = mybir.dt.float32
BF16 = mybir.dt.bfloat16
AF = mybir.ActivationFunctionType
ALU = mybir.AluOpType

D = 1024
KT = D // 128
D_IN = 2304
D_FF = 2816
NG_FF = D_FF // 256
N_HEADS = 4
HEAD_V = 64
HEAD_QK = 32
CONF_K = 31
EPS = 1e-6
C = 512
HALO = 32
TW = C + HALO
GW = 640
EW = 768
FAR_MIN = 113
N_CORES = 8
RING = 4
NT = 8


def lam_init_of(l):
    return 0.8 - 0.6 * math.exp(-0.3 * l)


def cpack_layout(depth):
    off = {}
    n = 0

    def add(name, w):
        nonlocal n
        off[name] = n
        n += w

    add("ones128", 128)
    add("bd32", 128)
    add("sel2", 128)
    add("bd64", 128)
    add("rb31", 4)
    add("invc", 32)
    for l in range(depth):
        for name, w in (("n1g", 8), ("n2g", 8), ("pscale", 2), ("sconv", 6), ("cdw", 62), ("cdb", 2),
                        ("clg", 2), ("clb", 2), ("qg0", 1), ("qg1", 1), ("kg", 1), ("sg", 1), ("lq1", 32), ("lk1", 32),
                        ("lq2", 32), ("lk2", 32), ("bd", 256), ("neglam", 1), ("ltmp", 36)):
            add(f"{name}{l}", w)
    return off, n


def rel_bucket_np(n):
    n = np.asarray(n, dtype=np.int64)
    nf = np.maximum(n, 1).astype(np.float32)
    large = 16 + (np.log(nf / np.float32(16)) / np.float32(math.log(128 / 16)) * np.float32(16)).astype(np.int32)
    large = np.minimum(large, 31)
    return np.where(n < 16, n, large)


def build_cpack(inp, depth):
    off, n = cpack_layout(depth)
    cp = np.zeros((128, n), np.float32)
    p = np.arange(128)
    cp[:, off["ones128"]:off["ones128"] + 128] = 1.0
    bd32 = (p[:, None] // 32 == p[None, :] // 32).astype(np.float32)
    cp[:, off["bd32"]:off["bd32"] + 128] = bd32
    cp[64, off["sel2"]:off["sel2"] + 64] = 1.0
    cp[63, off["sel2"] + 64:off["sel2"] + 128] = 1.0
    cp[:, off["bd64"]:off["bd64"] + 128] = (p[:, None] // 64 == p[None, :] // 64).astype(np.float32)
    cp[:, off["rb31"]:off["rb31"] + 4] = inp["rel_bias"][31][None, :]
    t = np.arange(16)
    for hf in range(2):
        win = np.where(p < 64, 2 ** (2 * hf + 1), 2 ** (2 * hf + 2))
        cp[:, off["invc"] + hf * 16: off["invc"] + hf * 16 + 16] = \
            1.0 / np.minimum(t[None, :] + 1, win[:, None]).astype(np.float32)
    for l in range(depth):
        o = lambda nm: off[f"{nm}{l}"]
        cp[:, o("n1g"):o("n1g") + 8] = inp["norm1_g"][l].reshape(8, 128).T
        cp[:, o("n2g"):o("n2g") + 8] = inp["norm2_g"][l].reshape(8, 128).T
        cp[:, o("pscale"):o("pscale") + 2] = inp["pool_scale"][l].reshape(2, 128).T
        cp[:, o("sconv"):o("sconv") + 6] = inp["sconv_w"][l].reshape(3, 2, 128).transpose(2, 1, 0).reshape(128, 6)
        cp[:, o("cdw"):o("cdw") + 62] = inp["conf_dw_w"][l].reshape(31, 2, 128).transpose(2, 1, 0).reshape(128, 62)
        cp[:, o("cdb"):o("cdb") + 2] = inp["conf_dw_b"][l].reshape(2, 128).T
        cp[:, o("clg"):o("clg") + 2] = inp["conf_ln_g"][l].reshape(2, 128).T
        cp[:, o("clb"):o("clb") + 2] = inp["conf_ln_b"][l].reshape(2, 128).T
        cp[:, o("qg0")] = np.where((p // 32) % 2 == 0, inp["q_norm_g"][l][p % 32], 0.0)
        cp[:, o("qg1")] = np.where((p // 32) % 2 == 1, inp["q_norm_g"][l][p % 32], 0.0)
        cp[:, o("kg")] = inp["k_norm_g"][l][p % 32]
        cp[:, o("sg")] = inp["subln_g"][l][p % 64]
        lamv = {"lq1": inp["lam_q1"], "lk1": inp["lam_k1"], "lq2": inp["lam_q2"], "lk2": inp["lam_k2"]}
        for nm in ("lq1", "lk1", "lq2", "lk2"):
            cp[:, o(nm):o(nm) + 32] = lamv[nm][l][None, :]
        for hf in range(2):
            for g in range(2):
                cp[g * 64:(g + 1) * 64, o("bd") + hf * 128 + g * 64: o("bd") + hf * 128 + (g + 1) * 64] = \
                    inp["pool_w"][l][2 * hf + g]
    return cp


def build_bias_tables(inp):
    oh = np.zeros((32, EW), np.float32)
    d = np.arange(EW - 127)
    b = rel_bucket_np(d)
    oh[b, d + 127] = 1.0
    rbrep = np.repeat(inp["rel_bias"].astype(np.float32)[:, :, None], 128, axis=2).reshape(32, 4 * 128)
    return oh, rbrep


class Op:
    __slots__ = ("eng", "fn", "deps", "needs", "sem", "val", "is_dma", "label")


class Prog:
    ENGS = ("pe", "act", "dve", "pool", "sp")

    def __init__(self):
        self.ops = {e: [] for e in self.ENGS}
        self.lastw = {}
        self.readers = {}
        self.dma_count = {}
        self.label = "setup"

    def op(self, eng, fn, reads=(), writes=(), dma=None):
        o = Op()
        o.label = self.label
        o.eng, o.fn, o.deps, o.needs, o.sem, o.val = eng, fn, set(), False, None, None
        o.is_dma = dma is not None
        for k in reads:
            w = self.lastw.get(k)
            if w is not None:
                o.deps.add(w)
        for k in writes:
            w = self.lastw.get(k)
            if w is not None:
                o.deps.add(w)
            for r in self.readers.get(k, ()):
                o.deps.add(r)
        if eng == "pe":
            o.deps = {d for d in o.deps if d.eng != "pe" or d.is_dma}
        for k in reads:
            self.readers.setdefault(k, []).append(o)
        for k in writes:
            self.lastw[k] = o
            self.readers[k] = []
        for d in o.deps:
            d.needs = True
        if dma is not None:
            self.dma_count[dma] = self.dma_count.get(dma, 0) + 1
            o.sem, o.val = dma, 16 * self.dma_count[dma]
            o.needs = True
        self.ops[eng].append(o)
        return o

    def fence(self, src_keys, dst_keys):
        users = []
        for k in src_keys:
            w = self.lastw.get(k)
            if w is not None:
                users.append(w)
            users.extend(self.readers.get(k, ()))
        for k in dst_keys:
            self.readers.setdefault(k, []).extend(users)

    def finalize(self):
        for e in self.ENGS:
            n = 0
            for o in self.ops[e]:
                if o.is_dma:
                    continue
                if o.needs:
                    n += 1
                    o.sem, o.val = "c_" + e, n

    def emit(self, eng, engine, sems, final_waits=()):
        waited = {}
        for o in self.ops[eng]:
            need = {}
            for d in o.deps:
                if need.get(d.sem, 0) < d.val:
                    need[d.sem] = d.val
            for s, v in need.items():
                if waited.get(s, 0) < v:
                    engine.wait_ge(sems[s], v)
                    waited[s] = v
            ins = o.fn(engine)
            if o.needs:
                ins.then_inc(sems[o.sem], 16 if o.is_dma else 1)
        for s, v in final_waits:
            engine.wait_ge(sems[s], v)


def build_program(S, NSEQ, DEPTH):
    NCH = S // C
    NTT = S // 128
    off, NC = cpack_layout(DEPTH)
    nc = bass.Bass("TRN2", target_bir_lowering=False)
    dr = {}
    dr["xT"] = nc.dram_tensor("xT", [NSEQ, D, S], F32, kind="ExternalInput")
    dr["cpack"] = nc.dram_tensor("cpack", [128, NC], F32, kind="ExternalInput")
    dr["oh"] = nc.dram_tensor("oh", [32, EW], F32, kind="ExternalInput")
    dr["rbrep"] = nc.dram_tensor("rbrep", [32, 512], F32, kind="ExternalInput")
    dr["w_in"] = nc.dram_tensor("w_in", [DEPTH, D, D_IN], F32, kind="ExternalInput")
    dr["w_out"] = nc.dram_tensor("w_out", [DEPTH, D, D], F32, kind="ExternalInput")
    dr["w_gate"] = nc.dram_tensor("w_gate", [DEPTH, D, D_FF], F32, kind="ExternalInput")
    dr["w_up"] = nc.dram_tensor("w_up", [DEPTH, D, D_FF], F32, kind="ExternalInput")
    dr["w_down"] = nc.dram_tensor("w_down", [DEPTH, D_FF, D], F32, kind="ExternalInput")
    dr["yT"] = nc.dram_tensor("yT", [NSEQ, D, S], F32, kind="ExternalOutput")
    dr["scr"] = nc.dram_tensor("scr", [4, 128 * EW], F32, kind="Internal")

    P = Prog()
    st = ExitStack()
    sb = lambda name, shape: st.enter_context(nc.sbuf_tensor(name, shape, F32))
    xT = sb("xT_sb", [128, KT, C])
    hT = sb("hT_sb", [128, KT, C])
    big = sb("big_sb", [128, 12, C])
    NP = 6
    AW = NT * TW + 2 * 2 * C + NP * C
    arena = sb("arena", [128, AW])
    arena_bf = arena.bitcast(BF16)
    Tt = [arena[:, i * TW:(i + 1) * TW] for i in range(NT)]
    qz = arena[:, NT * TW: NT * TW + 4 * C].rearrange("p (a m t) -> p a m t", a=2, m=2)
    Pt = [arena[:, NT * TW + 4 * C + i * C: NT * TW + 4 * C + (i + 1) * C] for i in range(NP)]
    NWOP = 3
    WH = [arena_bf[:, j * 4096: j * 4096 + 2048] for j in range(NWOP)]
    WL = [arena_bf[:, j * 4096 + 2048: (j + 1) * 4096] for j in range(NWOP)]
    FT = [arena[:, NWOP * 2048 + i * C: NWOP * 2048 + (i + 1) * C] for i in range(6)]
    assert NWOP * 2048 + 6 * C <= AW
    MIX_KEYS = [("T", i) for i in range(NT)] + [("qn", a, m) for a in range(2) for m in range(2)] + \
               [("P", i) for i in range(NP)]
    FFN_KEYS = [("wop", j) for j in range(NWOP)] + [("FT", i) for i in range(6)]
    Kc = [sb(f"Kc{l}", [128, 2, S]) for l in range(DEPTH)]
    VROW = N_HEADS * (HEAD_V + 1)
    Vx = [sb(f"Vx{l}", [128, NTT * VROW + 64]) for l in range(DEPTH)]
    Vx4 = [v[:, 0:NTT * VROW].rearrange("p (t h d) -> p t h d", t=NTT, h=N_HEADS) for v in Vx]
    G = sb("G_sb", [128, N_HEADS, GW])
    hT_bf = hT.bitcast(BF16)
    big_bf = big.bitcast(BF16)
    cp = sb("cp_sb", [128, NC])
    hal = [[sb(f"hal{l}_{i}", [128, HALO]) for i in range(6)] for l in range(DEPTH)]
    ring = [sb(f"ring{i}", [128, 2048]) for i in range(RING)]
    ps = [st.enter_context(nc.psum_tensor(f"ps{i}", [128, C], F32)) for i in range(8)]

    sem_names = ["c_pe", "c_act", "c_dve", "c_pool", "c_sp", "d_setup", "d_scr0", "d_scr1"] + \
                [f"d_ring{i}" for i in range(RING)] + [f"d_x{k}" for k in range(KT)] + [f"d_y{k}" for k in range(KT)]
    sems = {n: st.enter_context(nc.semaphore(n)) for n in sem_names}

    cpc = lambda name, l=None, i=0, w=1: cp[:, off[name if l is None else f"{name}{l}"] + i:
                                              off[name if l is None else f"{name}{l}"] + i + w]
    ones128 = cpc("ones128", w=128)
    bd32 = cpc("bd32", w=128)

    state = {"ring": 0, "ps": 0}

    def ps_next(pool=(0, 1, 2, 3, 4, 5, 6, 7)):
        i = pool[state["ps"] % len(pool)]
        state["ps"] += 1
        return i

    def mm(out, lhsT, rhs, start, stop, reads, writes, tp=None):
        kw = {} if tp is None else {"tile_position": tp}
        return P.op("pe", lambda e: e.matmul(out, lhsT, rhs, start=start, stop=stop, **kw), reads, writes)

    def act(out, in_, func, reads, writes, bias=None, scale=None):
        kw = {}
        if bias is not None:
            kw["bias"] = bias
        if scale is not None:
            kw["scale"] = scale
        return P.op("act", lambda e: e.activation(out, in_, func, **kw), reads, writes)

    def acopy(out, in_, reads, writes):
        return P.op("act", lambda e: e.copy(out, in_), reads, writes)

    def tt(out, in0, in1, op, reads, writes):
        return P.op("dve", lambda e: e.tensor_tensor(out, in0, in1, op), reads, writes)

    def ts(out, in0, s1, s2, op0, op1, reads, writes):
        if op1 is None:
            return P.op("dve", lambda e: e.tensor_scalar(out, in0, s1, None, op0), reads, writes)
        return P.op("dve", lambda e: e.tensor_scalar(out, in0, s1, s2, op0, op1), reads, writes)

    def rsq(out, in_, c, reads, writes):
        P.op("act", lambda e: e.activation(out, in_, AF.Sqrt, bias=c), reads, writes)
        return P.op("dve", lambda e: e.reciprocal(out, out), writes, writes)

    def stt(out, in0, scalar, in1, op0, op1, reads, writes):
        return P.op("dve", lambda e: e.scalar_tensor_tensor(out, in0, scalar, in1, op0, op1), reads, writes)

    def gstt(out, in0, scalar, in1, op0, op1, reads, writes):
        return P.op("pool", lambda e: e.scalar_tensor_tensor(out, in0, scalar, in1, op0, op1), reads, writes)

    def gts(out, in0, s1, s2, op0, op1, reads, writes):
        return P.op("pool", lambda e: e.tensor_scalar(out, in0, s1, s2, op0, op1), reads, writes)

    def vcopy(out, in_, reads, writes):
        return P.op("dve", lambda e: e.tensor_copy(out, in_), reads, writes)

    def vmemset(ap, val, writes):
        return P.op("dve", lambda e: e.memset(ap, val), (), writes)

    def wload(src):
        i = state["ring"] % RING
        state["ring"] += 1
        return i, src

    def ring_dma(dst_fn, src):
        i = state["ring"] % RING
        state["ring"] += 1
        dst = dst_fn(ring[i])
        P.op("sp", lambda e: e.dma_start(out=dst, in_=src), (), [("ring", i)], dma=f"d_ring{i}")
        return i

    r3 = lambda t: t[:, :].rearrange("p (a n) -> p a n", n=256)
    r2 = lambda t: t[:, :].rearrange("p (a n) -> p a n", n=1024)

    P.op("sp", lambda e: e.dma_start(out=cp[:, :], in_=dr["cpack"].ap()), (), ["cp"], dma="d_setup")
    ohs = big[0:32, 0, :]
    ohs2 = big[0:32, 1, 0:256]
    rbs = big[0:32, 2, :]
    ohd = dr["oh"].ap()
    P.op("sp", lambda e: e.dma_start(out=ohs, in_=ohd[:, 0:512]), (), [("big", 0)], dma="d_setup")
    P.op("sp", lambda e: e.dma_start(out=ohs2, in_=ohd[:, 512:768]), (), [("big", 1)], dma="d_setup")
    P.op("sp", lambda e: e.dma_start(out=rbs, in_=dr["rbrep"].ap()), (), [("big", 2)], dma="d_setup")
    n_setup = P.dma_count["d_setup"]
    for o in P.ops["sp"]:
        o.val = 16 * n_setup

    for l in range(DEPTH):
        li = lam_init_of(l)
        ts(cpc("n1g", l, w=8), cpc("n1g", l, w=8), 32.0, None, ALU.mult, None, ["cp"], ["cp"])
        ts(cpc("n2g", l, w=8), cpc("n2g", l, w=8), 32.0, None, ALU.mult, None, ["cp"], ["cp"])
        ts(cpc("clg", l, w=2), cpc("clg", l, w=2), 16.0, None, ALU.mult, None, ["cp"], ["cp"])
        ts(cpc("kg", l), cpc("kg", l), math.sqrt(32.0), None, ALU.mult, None, ["cp"], ["cp"])
        ts(cpc("sg", l), cpc("sg", l), 8.0 * (1.0 - li), None, ALU.mult, None, ["cp"], ["cp"])
        tmp = cpc("ltmp", l, 4, 32)
        s1 = cpc("ltmp", l, 0)
        s2 = cpc("ltmp", l, 1)
        tt(tmp, cpc("lq1", l, w=32), cpc("lk1", l, w=32), ALU.mult, ["cp"], ["cp"])
        P.op("dve", lambda e, s1=s1, tmp=tmp: e.reduce_sum(s1, tmp, mybir.AxisListType.X), ["cp"], ["cp"])
        tt(tmp, cpc("lq2", l, w=32), cpc("lk2", l, w=32), ALU.mult, ["cp"], ["cp"])
        P.op("dve", lambda e, s2=s2, tmp=tmp: e.reduce_sum(s2, tmp, mybir.AxisListType.X), ["cp"], ["cp"])
        act(s1, s1, AF.Exp, ["cp"], ["cp"])
        act(s2, s2, AF.Exp, ["cp"], ["cp"])
        tt(s1, s2, s1, ALU.subtract, ["cp"], ["cp"])
        ts(cpc("neglam", l), s1, -li, None, ALU.add, None, ["cp"], ["cp"])
        vmemset(Vx[l][:, :], 0.0, [("Vx1", l)])
        vmemset(Vx4[l][:, :, :, HEAD_V:HEAD_V + 1], 1.0, [("Vx1", l)])

    for h in range(N_HEADS):
        eb = big[:, 4 + 2 * h: 6 + 2 * h, :].rearrange("p a n -> p (a n)")
        b0, b1 = ps_next(), ps_next()
        lhs = big[0:32, 2, h * 128:(h + 1) * 128]
        mm(ps[b0][:, 0:512], lhs, ohs, True, True, ["cp", ("big", 0), ("big", 2)], [("ps", b0)])
        mm(ps[b1][:, 0:256], lhs, ohs2, True, True, ["cp", ("big", 1), ("big", 2)], [("ps", b1)])
        act(eb[:, 0:512], ps[b0][:, 0:512], AF.Exp, [("ps", b0)], [("eb", h)])
        act(eb[:, 512:768], ps[b1][:, 0:256], AF.Exp, [("ps", b1)], [("eb", h)])
        vmemset(eb[:, 0:127], 0.0, [("eb", h)])
        scr_lin = bass.AP(dr["scr"], h * 128 * EW, [[EW, 128], [1, EW]])
        P.op("sp", lambda e, scr_lin=scr_lin, eb=eb: e.dma_start(out=scr_lin, in_=eb[:, 0:EW]),
             [("eb", h)], [("scr", h)], dma="d_scr0")
        scr_toe = bass.AP(dr["scr"], h * 128 * EW + 127, [[EW - 1, 128], [1, GW]])
        P.op("sp", lambda e, scr_toe=scr_toe, h=h: e.dma_start(out=G[:, h, :], in_=scr_toe),
             [("scr", h)], [("G", h)], dma="d_scr1")
    for o in P.ops["sp"]:
        if o.sem == "d_scr1":
            o.val = 16 * P.dma_count["d_scr1"]
    for h in range(N_HEADS):
        for j in (4 + 2 * h, 5 + 2 * h):
            P.lastw[("big", j)] = P.lastw[("scr", h)]
            P.readers[("big", j)] = []

    def rmsnorm_to_hT(gname, l, split=False):
        if split:
            tm = [(FT[i], ("FT", i)) for i in range(5)]
        else:
            tm = [(Tt[i][:, 0:C], ("T", i)) for i in range(3)]
        b = ps_next()
        for kt in range(KT):
            sq, sk = tm[kt % 2]
            act(sq, xT[:, kt, :], AF.Square, [("xT", kt)], [sk])
            mm(ps[b][:, :], ones128, sq, kt == 0, kt == KT - 1, ["cp", sk], [("ps", b)])
        rs, rk = tm[2]
        rsq(rs, ps[b][:, :], D * EPS, [("ps", b)], [rk])
        for kt in range(KT):
            if not split:
                stt(hT[:, kt, :], xT[:, kt, :], cpc(gname, l, kt), rs, ALU.mult, ALU.mult,
                    ["cp", ("xT", kt), rk], [("hT", kt)])
            else:
                h32, hk = tm[3 + kt % 2]
                stt(h32, xT[:, kt, :], cpc(gname, l, kt), rs, ALU.mult, ALU.mult, ["cp", ("xT", kt), rk], [hk])
                acopy(hT_bf[:, kt, 0:C], h32, [hk], [("hT", kt)])
                tt(hT_bf[:, kt, C:2 * C], h32, hT_bf[:, kt, 0:C], ALU.subtract, [hk, ("hT", kt)], [("hT", kt)])

    def proj_fm(slot, half, bank):
        W = r3(ring[slot])
        for kt in range(KT):
            mm(ps[bank][:, :], W[:, kt, half * 128:(half + 1) * 128], hT[:, kt, :], kt == 0, kt == KT - 1,
               [("ring", slot), ("hT", kt)], [("ps", bank)])

    def win_src(l, gi):
        return dr["w_in"].ap()[l].rearrange("(kt p) n -> p kt n", p=128)[:, :, gi * 256:(gi + 1) * 256]

    def halo_in(l, hi, tile_i, c):
        t = Tt[tile_i]
        if c == 0:
            vmemset(t[:, 0:HALO], 0.0, [("T", tile_i)])
        else:
            vcopy(t[:, 0:HALO], hal[l][hi][:, :], [("hal", l, hi)], [("T", tile_i)])

    def halo_out(l, hi, tile_i):
        vcopy(hal[l][hi][:, :], Tt[tile_i][:, C:C + HALO], [("T", tile_i)], [("hal", l, hi)])

    def mixer_half(l, s, c):
        P.fence(FFN_KEYS, MIX_KEYS)
        P.label = "norm1"
        rmsnorm_to_hT("n1g", l)
        P.label = "pool"
        U, PA, PB = 3, 4, 5
        PLs = (6, 2)
        slot = ring_dma(r3, win_src(l, 0))
        for hf in range(2):
            b = ps_next()
            proj_fm(slot, hf, b)
            halo_in(l, hf, U, c)
            acopy(Tt[U][:, HALO:TW], ps[b][:, :], [("ps", b)], [("T", U)])
            halo_out(l, hf, U)
            u = Tt[U]
            a_, b_ = Tt[PA], Tt[PB]
            tt(a_[:, 1:TW], u[:, 1:TW], u[:, 0:TW - 1], ALU.add, [("T", U)], [("T", PA)])
            tt(b_[:, 3:TW], a_[:, 3:TW], a_[:, 1:TW - 2], ALU.add, [("T", PA)], [("T", PB)])
            if hf == 0:
                wlo, whi = 2.0, 4.0
            else:
                tt(a_[:, 7:TW], b_[:, 7:TW], b_[:, 3:TW - 4], ALU.add, [("T", PB)], [("T", PA)])
                tt(b_[:, 15:TW], a_[:, 15:TW], a_[:, 7:TW - 8], ALU.add, [("T", PA)], [("T", PB)])
                wlo, whi = 8.0, 16.0
            lo, hi_ = a_, b_
            PL = PLs[hf]
            pl = Tt[PL]
            stt(pl[0:64, 0:C], lo[0:64, HALO:TW], 1.0 / wlo, u[0:64, HALO:TW], ALU.mult, ALU.subtract,
                [("T", PA), ("T", PB), ("T", U)], [("T", PL)])
            stt(pl[64:128, 0:C], hi_[64:128, HALO:TW], 1.0 / whi, u[64:128, HALO:TW], ALU.mult, ALU.subtract,
                [("T", PA), ("T", PB), ("T", U)], [("T", PL)])
            if c == 0:
                ic = cp[:, off["invc"] + hf * 16: off["invc"] + hf * 16 + 16]
                for (r0, r1, src) in ((0, 64, lo), (64, 128, hi_)):
                    tt(pl[r0:r1, 0:16], src[r0:r1, HALO:HALO + 16], ic[r0:r1, :], ALU.mult,
                       ["cp", ("T", PA), ("T", PB)], [("T", PL)])
                    tt(pl[r0:r1, 0:16], pl[r0:r1, 0:16], u[r0:r1, HALO:HALO + 16], ALU.subtract,
                       [("T", U), ("T", PL)], [("T", PL)])
        P.label = "sconv"
        SB, SC, V, TA = (0, 1), (4, 5), 3, 6
        slot = ring_dma(r3, win_src(l, 1))
        for hf in range(2):
            b = ps_next()
            proj_fm(slot, hf, b)
            acopy(Tt[SB[hf]][:, 0:C], ps[b][:, :], [("ps", b)], [("T", SB[hf])])
        P.label = "sconv"
        slot = ring_dma(r3, win_src(l, 2))
        for hf in range(2):
            b = ps_next()
            proj_fm(slot, hf, b)
            acopy(Tt[SC[hf]][:, 0:C], ps[b][:, :], [("ps", b)], [("T", SC[hf])])
        P.label = "pool"
        for hf in range(2):
            b2 = ps_next()
            mm(ps[b2][:, :], cpc("bd", l, hf * 128, 128), Tt[PLs[hf]][:, 0:C], True, True,
               ["cp", ("T", PLs[hf])], [("ps", b2)])
            act(big[:, hf, :], ps[b2][:, :], AF.Identity, ["cp", ("ps", b2)], [("big", hf)],
                scale=cpc("pscale", l, hf))
        P.label = "sconv"
        slot = ring_dma(r3, win_src(l, 3))
        for hf in range(2):
            b = ps_next()
            proj_fm(slot, hf, b)
            halo_in(l, 2 + hf, V, c)
            v = Tt[V]
            tt(v[:, HALO:TW], Tt[SC[hf]][:, 0:C], ps[b][:, :], ALU.mult, [("T", SC[hf]), ("ps", b)], [("T", V)])
            halo_out(l, 2 + hf, V)
            ta = Tt[TA]
            w = lambda j: cpc("sconv", l, hf * 3 + j)
            ts(ta[:, 0:C], v[:, HALO - 2:TW - 2], w(0), None, ALU.mult, None, ["cp", ("T", V)], [("T", TA)])
            stt(ta[:, 0:C], v[:, HALO - 1:TW - 1], w(1), ta[:, 0:C], ALU.mult, ALU.add,
                ["cp", ("T", V), ("T", TA)], [("T", TA)])
            stt(ta[:, 0:C], v[:, HALO:TW], w(2), ta[:, 0:C], ALU.mult, ALU.add,
                ["cp", ("T", V), ("T", TA)], [("T", TA)])
            tt(big[:, 2 + hf, :], ta[:, 0:C], Tt[SB[hf]][:, 0:C], ALU.mult,
               [("T", TA), ("T", SB[hf])], [("big", 2 + hf)])
        P.label = "conf"
        CA, SIG, CUs, ACC = (0, 1), 4, (3, 2), (5, 6)
        conv_taps = []
        slot = ring_dma(r3, win_src(l, 4))
        for hf in range(2):
            b = ps_next()
            proj_fm(slot, hf, b)
            acopy(Tt[CA[hf]][:, 0:C], ps[b][:, :], [("ps", b)], [("T", CA[hf])])
        slot = ring_dma(r3, win_src(l, 5))
        for hf in range(2):
            b = ps_next()
            proj_fm(slot, hf, b)
            act(Tt[SIG][:, 0:C], ps[b][:, :], AF.Sigmoid, [("ps", b)], [("T", SIG)])
            CU = CUs[hf]
            halo_in(l, 4 + hf, CU, c)
            cu = Tt[CU]
            tt(cu[:, HALO:TW], Tt[SIG][:, 0:C], Tt[CA[hf]][:, 0:C], ALU.mult,
               [("T", SIG), ("T", CA[hf])], [("T", CU)])
            halo_out(l, 4 + hf, CU)
            acc = Tt[ACC[hf]]

            def tap(j, hf=hf, cu=cu, acc=acc, CU=CU):
                src = cu[:, HALO - (CONF_K - 1) + j: HALO - (CONF_K - 1) + j + C]
                wj = cpc("cdw", l, hf * 31 + j)
                if j == 0:
                    ts(acc[:, 0:C], src, wj, cpc("cdb", l, hf), ALU.mult, ALU.add,
                       ["cp", ("T", CU)], [("T", ACC[hf])])
                else:
                    stt(acc[:, 0:C], src, wj, acc[:, 0:C], ALU.mult, ALU.add,
                        ["cp", ("T", CU), ("T", ACC[hf])], [("T", ACC[hf])])
            conv_taps.extend([(lambda j=j, tap=tap: tap(j)) for j in range(CONF_K)])
        def emit_taps(n):
            lab = P.label
            P.label = "conf"
            for _ in range(n):
                if conv_taps:
                    conv_taps.pop(0)()
            P.label = lab

        emit_taps(16)
        P.label = "qk"
        SQs, Rr = (4, 7), 4
        for gi, gname in ((6, "q"), (7, "k")):
            slot = ring_dma(r3, win_src(l, gi))
            banks = []
            for hf in range(2):
                b = ps_next()
                proj_fm(slot, hf, b)
                act(Tt[SQs[hf]][:, 0:C], ps[b][:, :], AF.Square, [("ps", b)], [("T", SQs[hf])])
                banks.append(b)
            for hf in range(2):
                b = banks[hf]
                SQ = SQs[hf]
                b2 = ps_next()
                mm(ps[b2][:, :], bd32, Tt[SQ][:, 0:C], True, True, ["cp", ("T", SQ)], [("ps", b2)])
                rsq(Tt[SQ][:, 0:C], ps[b2][:, :], 32.0 * EPS, [("ps", b2)], [("T", SQ)])
                if gi == 6:
                    for m in range(2):
                        stt(qz[:, hf, m, :], ps[b][:, :], cpc(f"qg{m}", l), Tt[SQ][:, 0:C], ALU.mult, ALU.mult,
                            ["cp", ("ps", b), ("T", SQ)], [("qn", hf, m)])
                else:
                    stt(Kc[l][:, hf, c * C:(c + 1) * C], ps[b][:, :], cpc("kg", l), Tt[SQ][:, 0:C],
                        ALU.mult, ALU.mult, ["cp", ("ps", b), ("T", SQ)], [("Kc", l, hf, c)])
            emit_taps(8 if gi == 6 else 30)
        slot = ring_dma(r3, win_src(l, 8))
        W = r3(ring[slot])

        def vproj(tti):
            P.label = "v"
            b = ps_next()
            for kt in range(KT):
                mm(ps[b][:, 0:256], hT[:, kt, tti * 128:(tti + 1) * 128], W[:, kt, :], kt == 0, kt == KT - 1,
                   [("ring", slot), ("hT", kt)], [("ps", b)])
            idx = c * (C // 128) + tti
            acopy(Vx4[l][:, idx, :, 0:HEAD_V], ps[b][:, 0:256].rearrange("p (h d) -> p h d", h=N_HEADS),
                  [("ps", b)], [("Vx", l, idx)])

        for tti in range(C // 128):
            vproj(tti)
        P.label = "attn"
        OS, RR = (0, 1), (4, 7)
        bO = (4, 5)
        nk = 4 * c + 4
        steps = [(h, kt) for h in range(N_HEADS) for kt in range(nk)]

        def geom(kt):
            j0 = max(0, kt - 4 * c) * 128
            return j0, C - j0

        def qk(i):
            P.label = f"attn_qk_c{c}"
            h, kt = steps[i]
            qt, hb = h // 2, (h % 2) * 64
            j0, n = geom(kt)
            far = kt <= 4 * c - 2
            delta = C * c - 128 * kt
            for m in range(2):
                bS = ps_next((0, 1, 2, 3))
                pi = (2 * i + m) % NP
                mm(ps[bS][:, 0:n], Kc[l][hb:hb + 64, qt, kt * 128:(kt + 1) * 128], qz[hb:hb + 64, qt, m, j0:C],
                   True, True, [("Kc", l, qt, kt // 4), ("qn", qt, m)], [("ps", bS)])
                if far:
                    act(Pt[pi][:, 0:n], ps[bS][:, 0:n], AF.Exp, ["cp", ("ps", bS)], [("P", pi)],
                        bias=cpc("rb31", None, h))
                else:
                    act(Pt[pi][:, 0:n], ps[bS][:, 0:n], AF.Exp, [("ps", bS)], [("P", pi)])
                    g0 = delta + j0
                    tt(Pt[pi][:, 0:n], Pt[pi][:, 0:n], G[:, h, g0:g0 + n], ALU.mult,
                       [("P", pi), ("G", h)], [("P", pi)])

        def av(i):
            P.label = f"attn_av_c{c}"
            h, kt = steps[i]
            j0, n = geom(kt)
            base = kt * VROW + (h * (HEAD_V + 1) if h % 2 == 0 else (h - 1) * (HEAD_V + 1) + 1)
            for m in range(2):
                pi = (2 * i + m) % NP
                mm(ps[bO[m]][:, j0:C], Vx[l][:, base:base + 128], Pt[pi][:, 0:n], kt == 0, kt == nk - 1,
                   [("Vx", l, kt), ("Vx1", l), ("P", pi)], [("ps", bO[m])])

        def tail_a1(h):
            for m in range(2):
                acopy(Tt[OS[m]][:, 0:C], ps[bO[m]][:, :], [("ps", bO[m])], [("T", OS[m])])

        def rows(h):
            return (0, 64) if h % 2 == 0 else (64, 128)

        def tail_a2(h):
            P.label = f"attn_tailA_c{c}"
            r0, r1 = rows(h)
            for m in range(2):
                bD = ps_next((6, 7))
                mm(ps[bD][:, :], cpc("sel2", w=128), Tt[OS[m]][:, 0:C], True, True, ["cp", ("T", OS[m])], [("ps", bD)])
                P.op("dve", lambda e, o=Tt[RR[m]][r0:r1, 0:C], i=ps[bD][r0:r1, :]: e.reciprocal(o, i),
                     [("ps", bD)], [("T", RR[m])])
                tt(Tt[RR[m]][r0:r1, 0:C], Tt[OS[m]][r0:r1, 0:C], Tt[RR[m]][r0:r1, 0:C], ALU.mult,
                   [("T", OS[m]), ("T", RR[m])], [("T", RR[m])])
            stt(Tt[RR[0]][r0:r1, 0:C], Tt[RR[1]][r0:r1, 0:C], cp[r0:r1, off[f"neglam{l}"]:off[f"neglam{l}"] + 1],
                Tt[RR[0]][r0:r1, 0:C], ALU.mult, ALU.add, ["cp", ("T", RR[1]), ("T", RR[0])], [("T", RR[0])])
            act(Tt[OS[0]][r0:r1, 0:C], Tt[RR[0]][r0:r1, 0:C], AF.Square, [("T", RR[0])], [("T", OS[0])])

        def tail_b(h, pool=(6, 7)):
            P.label = f"attn_tailB_c{c}"
            r0, r1 = rows(h)
            bq = ps_next(pool)
            mm(ps[bq][:, :], cpc("bd64", w=128), Tt[OS[0]][:, 0:C], True, True, ["cp", ("T", OS[0])], [("ps", bq)])
            rsq(Tt[OS[1]][r0:r1, 0:C], ps[bq][r0:r1, :], 64.0 * EPS, [("ps", bq)], [("T", OS[1])])
            stt(big[r0:r1, 6 + h // 2, :], Tt[RR[0]][r0:r1, 0:C], cp[r0:r1, off[f"sg{l}"]:off[f"sg{l}"] + 1],
                Tt[OS[1]][r0:r1, 0:C], ALU.mult, ALU.mult, ["cp", ("T", RR[0]), ("T", OS[1])],
                [("big", 6 + h // 2)])

        def run_pending(pending, upto):
            keep = []
            for (when, fn, hh) in pending:
                if upto is None or when <= upto:
                    fn(hh)
                else:
                    keep.append((when, fn, hh))
            return keep

        SQc = CUs

        def confln_part1():
            lab = P.label
            emit_taps(2 * CONF_K)
            P.label = "confln"
            bm = ps_next((6, 7))
            for hf in range(2):
                mm(ps[bm][:, :], ones128, Tt[ACC[hf]][:, 0:C], hf == 0, hf == 1, ["cp", ("T", ACC[hf])],
                   [("ps", bm)])
            for hf in range(2):
                stt(Tt[ACC[hf]][:, 0:C], ps[bm][:, :], -1.0 / 256.0, Tt[ACC[hf]][:, 0:C], ALU.mult, ALU.add,
                    [("ps", bm), ("T", ACC[hf])], [("T", ACC[hf])])
                act(Tt[SQc[hf]][:, 0:C], Tt[ACC[hf]][:, 0:C], AF.Square, [("T", ACC[hf])], [("T", SQc[hf])])
            P.label = lab

        def confln_part2():
            lab = P.label
            P.label = "confln"
            bv = ps_next((6, 7))
            for hf in range(2):
                mm(ps[bv][:, :], ones128, Tt[SQc[hf]][:, 0:C], hf == 0, hf == 1, ["cp", ("T", SQc[hf])],
                   [("ps", bv)])
            RSTD = SQc[0]
            rsq(Tt[RSTD][:, 0:C], ps[bv][:, :], 256.0 * EPS, [("ps", bv)], [("T", RSTD)])
            for hf in range(2):
                tt(Tt[ACC[hf]][:, 0:C], Tt[ACC[hf]][:, 0:C], Tt[RSTD][:, 0:C], ALU.mult,
                   [("T", ACC[hf]), ("T", RSTD)], [("T", ACC[hf])])
                act(big[:, 4 + hf, :], Tt[ACC[hf]][:, 0:C], AF.Silu, ["cp", ("T", ACC[hf])], [("big", 4 + hf)],
                    bias=cpc("clb", l, hf), scale=cpc("clg", l, hf))
            P.label = lab

        pending = []
        TAPS_PER_STEP = 1
        qk(0)
        qk(1)
        for i in range(len(steps)):
            if i + 2 < len(steps):
                qk(i + 2)
            emit_taps(TAPS_PER_STEP)
            av(i)
            if i == 10:
                confln_part1()
            if i == 13:
                confln_part2()
            pending = run_pending(pending, i)
            h, kt = steps[i]
            if kt == nk - 1:
                tail_a1(h)
                pending.append((i + 1, tail_a2, h))
                pending.append((i + min(nk, 7), tail_b, h))
        last_b = [(w_, f_, h_) for (w_, f_, h_) in pending if f_ is tail_b]
        pending = run_pending([(w_, f_, h_) for (w_, f_, h_) in pending if f_ is not tail_b], None)
        P.label = "wout"
        wo = dr["w_out"].ap()[l]

        def wout_stage(groups):
            nK = 2 * len(groups)
            ki = 0
            for g in groups:
                slot = ring_dma(r2, wo[g * 256:(g + 1) * 256, :].rearrange("(a p) n -> p a n", p=128))
                W = r2(ring[slot])
                for a in range(2):
                    for m in range(KT):
                        mm(ps[m][:, :], W[:, a, m * 128:(m + 1) * 128], big[:, 2 * g + a, :], ki == 0, ki == nK - 1,
                           [("ring", slot), ("big", 2 * g + a)], [("ps", m)])
                    ki += 1
            for m in range(KT):
                tt(xT[:, m, :], ps[m][:, :], xT[:, m, :], ALU.add, [("ps", m), ("xT", m)], [("xT", m)])

        wout_stage(range(0, 3))
        P.label = "attn"
        for (w_, f_, h_) in last_b:
            f_(h_, pool=(1,))
        P.label = "wout"
        wout_stage(range(3, 4))

    wop_state = {"n": 0}

    def wsplit(rslot):
        j = wop_state["n"] % NWOP
        wop_state["n"] += 1
        acopy(WH[j], ring[rslot][:, :], [("ring", rslot)], [("wop", j)])
        tt(WL[j], ring[rslot][:, :], WH[j], ALU.subtract, [("ring", rslot), ("wop", j)], [("wop", j)])
        return j

    COMBOS = ((0, 0), (0, 1), (1, 0))

    def ffn_half(l):
        P.fence(MIX_KEYS, FFN_KEYS)
        P.label = "norm2"
        rmsnorm_to_hT("n2g", l, split=True)
        wg = dr["w_gate"].ap()[l].rearrange("(kt p) n -> p kt n", p=128)
        wu = dr["w_up"].ap()[l].rearrange("(kt p) n -> p kt n", p=128)
        wd = dr["w_down"].ap()[l]
        v3 = lambda t: t.rearrange("p (a n) -> p a n", n=256)
        v2 = lambda t: t.rearrange("p (a n) -> p a n", n=1024)

        def projb(j, t, bank):
            Wp = (v3(WH[j]), v3(WL[j]))
            n = 0
            for kt in range(KT):
                for (wi, hi_) in COMBOS:
                    mm(ps[bank][:, :], Wp[wi][:, kt, t * 128:(t + 1) * 128], hT_bf[:, kt, hi_ * C:(hi_ + 1) * C],
                       n == 0, n == 3 * KT - 1, [("wop", j), ("hT", kt)], [("ps", bank)])
                    n += 1

        halves = (range(0, 6), range(6, NG_FF))
        seq = []
        for groups in halves:
            for g in groups:
                seq += [("g", g), ("u", g)]
            for g in groups:
                seq += [("d", g)]
        issued = {}
        nxt = [0]

        def ensure(k):
            lab = P.label
            while nxt[0] <= k and nxt[0] < len(seq):
                kind, g = seq[nxt[0]]
                P.label = "down" if kind == "d" else "gateup"
                if kind == "g":
                    r = ring_dma(r3, wg[:, :, g * 256:(g + 1) * 256])
                elif kind == "u":
                    r = ring_dma(r3, wu[:, :, g * 256:(g + 1) * 256])
                else:
                    r = ring_dma(r2, wd[g * 256:(g + 1) * 256, :].rearrange("(a p) n -> p a n", p=128))
                issued[nxt[0]] = wsplit(r)
                nxt[0] += 1
            P.label = lab

        ensure(1)
        idx = 0
        for groups in halves:
            g0 = groups[0]
            P.label = "gateup"
            for g in groups:
                jg = issued[idx]
                banks_g = []
                for t in range(2):
                    bg = ps_next()
                    projb(jg, t, bg)
                    banks_g.append(bg)
                ensure(idx + 2)
                for t in range(2):
                    act(FT[t], ps[banks_g[t]][:, :], AF.Silu, [("ps", banks_g[t])], [("FT", t)])
                ju = issued[idx + 1]
                banks_u = []
                for t in range(2):
                    bu = ps_next()
                    projb(ju, t, bu)
                    banks_u.append(bu)
                ensure(idx + 3)
                for t in range(2):
                    ai = (g - g0) * 2 + t
                    bu = banks_u[t]
                    tt(FT[2 + t], FT[t], ps[bu][:, :], ALU.mult, [("FT", t), ("ps", bu)], [("FT", 2 + t)])
                    acopy(big_bf[:, ai, 0:C], FT[2 + t], [("FT", 2 + t)], [("big", ai)])
                    tt(big_bf[:, ai, C:2 * C], FT[2 + t], big_bf[:, ai, 0:C], ALU.subtract,
                       [("FT", 2 + t), ("big", ai)], [("big", ai)])
                idx += 2
            P.label = "down"
            nK = 2 * len(groups) * 3
            ki = 0
            base = state["ps"] % 8
            bank_of = [(base + m) % 8 for m in range(KT)]
            for g in groups:
                j = issued[idx]
                Wp = (v2(WH[j]), v2(WL[j]))
                for a in range(2):
                    ai = (g - g0) * 2 + a
                    for (wi, xi) in COMBOS:
                        for m in range(KT):
                            bk = bank_of[m]
                            mm(ps[bk][:, :], Wp[wi][:, a, m * 128:(m + 1) * 128], big_bf[:, ai, xi * C:(xi + 1) * C],
                               ki == 0, ki == nK - 1, [("wop", j), ("big", ai)], [("ps", bk)])
                        ki += 1
                ensure(idx + 2)
                idx += 1
            for m in range(KT):
                bk = bank_of[m]
                tt(xT[:, m, :], ps[bk][:, :], xT[:, m, :], ALU.add, [("ps", bk), ("xT", m)], [("xT", m)])

    xin = dr["xT"].ap()
    yout = dr["yT"].ap()
    chunks = [(s_, c_) for s_ in range(NSEQ) for c_ in range(NCH)]

    def xload(s_, c_, kt):
        src = xin[s_][kt * 128:(kt + 1) * 128, c_ * C:(c_ + 1) * C]
        P.op("pool", lambda e: e.dma_start(out=xT[:, kt, :], in_=src), (), [("xT", kt)], dma=f"d_x{kt}")

    def ystore(s_, c_, kt):
        dst = yout[s_][kt * 128:(kt + 1) * 128, c_ * C:(c_ + 1) * C]
        P.op("pool", lambda e: e.dma_start(out=dst, in_=xT[:, kt, :]), [("xT", kt)], (), dma=f"d_y{kt}")

    P.label = "io"
    for kt in range(KT):
        xload(*chunks[0], kt)
    for ci, (s, c) in enumerate(chunks):
        for l in range(DEPTH):
            mixer_half(l, s, c)
            ffn_half(l)
        P.label = "io"
        for kt in range(KT):
            ystore(s, c, kt)
        if ci + 1 < len(chunks):
            for kt in range(KT):
                xload(*chunks[ci + 1], kt)

    P.finalize()
    build_program.last_prog = P
    with nc.Block() as block:
        @block.tensor
        def _(e):
            P.emit("pe", e, sems)

        @block.scalar
        def _(e):
            P.emit("act", e, sems)

        @block.vector
        def _(e):
            P.emit("dve", e, sems)

        @block.gpsimd
        def _(e):
            P.emit("pool", e, sems, final_waits=[(f"d_y{k}", 16 * P.dma_count[f"d_y{k}"]) for k in range(KT)])

        @block.sync
        def _(e):
            P.emit("sp", e, sems)
    st.close()
    return nc


def run(inputs, S, NSEQ, DEPTH, n_cores=N_CORES, trace=False):
    x = np.asarray(inputs["x"], np.float32)
    inp = {k: np.asarray(v, np.float32) for k, v in inputs.items()}
    cpack = build_cpack(inp, DEPTH)
    oh, rbrep = build_bias_tables(inp)
    nc = build_program(S, NSEQ, DEPTH)
    in_maps = []
    for i in range(n_cores):
        xs = np.ascontiguousarray(x[i * NSEQ:(i + 1) * NSEQ].transpose(0, 2, 1))
        in_maps.append({
            "xT": xs, "cpack": cpack, "oh": oh, "rbrep": rbrep,
            "w_in": np.ascontiguousarray(inp["w_in"][:DEPTH]), "w_out": np.ascontiguousarray(inp["w_out"][:DEPTH]),
            "w_gate": np.ascontiguousarray(inp["w_gate"][:DEPTH]), "w_up": np.ascontiguousarray(inp["w_up"][:DEPTH]),
            "w_down": np.ascontiguousarray(inp["w_down"][:DEPTH]),
        })
    res = run_bass_kernel_spmd(nc, in_maps, core_ids=list(range(n_cores)), trace=trace)
    out = np.concatenate([np.asarray(r["yT"]).transpose(0, 2, 1) for r in res.results], axis=0)
    return np.ascontiguousarray(out.astype(np.float32)), res


def kernel(**inputs):
    out, _ = run(inputs, S=2048, NSEQ=2, DEPTH=2)
    return out
```
